# Optimizing a Trainium2 kernel written in Bass

```python
import math
import jax, jax.numpy as jnp
from jax import lax
import numpy as np

D_MODEL = 1024
BATCH = 4
SEQ = 8192
DEPTH = 2

N_MIXERS = 2
N_A_LAYERS = (DEPTH + 1) // 2
N_B_LAYERS = DEPTH // 2

CHUNK = 128
GATE_WIDTH = D_MODEL
GATE_GROUPS = 8
GATE_GROUP_DIM = GATE_WIDTH // GATE_GROUPS

WINDOW_DILATIONS = ((128, 1), (512, 4), (2048, 16))
N_DIL_GROUPS = len(WINDOW_DILATIONS)
ATT_HEADS = 8
HEAD_DIM = 64
ATT_WIDTH = ATT_HEADS * HEAD_DIM

N_BUCKETS = 32
MAX_EXACT = N_BUCKETS // 2
REL_MAX_DISTANCE = max(w for w, _ in WINDOW_DILATIONS)

D_FF = 4 * D_MODEL

EPS = 1e-6
NEG_INF = -1e30

kernel_name = "interleaved_gmlp_dilated_attention_trunk"


def _rms_norm(x, g):
    xf = x.astype(jnp.float32)
    y = xf * lax.rsqrt(jnp.mean(xf * xf, axis=-1, keepdims=True) + EPS)
    return (y * g.astype(jnp.float32)).astype(x.dtype)


def _layer_norm(x, g, b):
    xf = x.astype(jnp.float32)
    mu = jnp.mean(xf, axis=-1, keepdims=True)
    xc = xf - mu
    y = xc * lax.rsqrt(jnp.mean(xc * xc, axis=-1, keepdims=True) + EPS)
    return (y * g.astype(jnp.float32) + b.astype(jnp.float32)).astype(x.dtype)


def _t5_bucket(distance):
    small = distance < MAX_EXACT
    nf = jnp.maximum(distance, 1).astype(jnp.float32)
    large = MAX_EXACT + (jnp.log(nf / MAX_EXACT) / math.log(REL_MAX_DISTANCE / MAX_EXACT)
                         * (N_BUCKETS - MAX_EXACT)).astype(jnp.int32)
    large = jnp.minimum(large, N_BUCKETS - 1)
    return jnp.where(small, distance, large)


def _chunk_gating_mixer(h, w_in, ln_g, ln_b, w_s, b_s, w_out):
    B_, S_, _ = h.shape
    uv = jax.nn.gelu(h @ w_in, approximate=False)
    u, v = jnp.split(uv, 2, axis=-1)
    v = _layer_norm(v, ln_g, ln_b)
    nc = S_ // CHUNK
    v = v.reshape(B_, nc, CHUNK, GATE_GROUPS, GATE_GROUP_DIM)
    causal = jnp.tril(jnp.ones((CHUNK, CHUNK), dtype=bool))
    w = jnp.where(causal[None], w_s, 0.0)
    mixed = jnp.einsum('gts,bnsgc->bntgc', w, v) + b_s.T[None, None, :, :, None]
    gate = mixed.reshape(B_, S_, GATE_WIDTH)
    return (u * gate) @ w_out


def _dilated_group(q, k, v, bias_table, window, dilation):
    B_, S_, H, hd = q.shape
    blk = window // dilation
    span = blk * dilation
    Sp = -(-S_ // span) * span
    nb = Sp // span

    def split(t):
        t = jnp.pad(t, ((0, 0), (0, Sp - S_), (0, 0), (0, 0)))
        return t.reshape(B_, nb, blk, dilation, H, hd)

    def with_prev(t):
        prev = jnp.pad(t, ((0, 0), (1, 0), (0, 0), (0, 0), (0, 0), (0, 0)))[:, :-1]
        return jnp.concatenate([prev, t], axis=2)

    qb = split(q)
    kc = with_prev(split(k))
    vc = with_prev(split(v))

    s = jnp.einsum('bnqrhc,bnkrhc->bnrhqk', qb, kc) * (HEAD_DIM ** -0.5)
    rel = blk + jnp.arange(blk)[:, None] - jnp.arange(2 * blk)[None, :]
    band = (rel >= 0) & (rel <= blk)
    bucket = _t5_bucket(jnp.clip(rel, 0, blk) * dilation)
    bias = jnp.transpose(bias_table[bucket], (2, 0, 1))
    first = (jnp.arange(nb)[:, None, None] == 0) & (jnp.arange(2 * blk)[None, None, :] < blk)
    valid = band[None] & ~first
    logits = jnp.where(valid[None, :, None, None], s + bias[None, None, None], NEG_INF)

    m = jnp.max(logits, axis=-1)
    p = jnp.exp(logits - m[..., None])
    den = jnp.sum(p, axis=-1)
    num = jnp.einsum('bnrhqk,bnkrhc->bnqrhc', p, vc)

    num = num.reshape(B_, Sp, H, hd)[:, :S_]
    den = jnp.transpose(den, (0, 1, 4, 2, 3)).reshape(B_, Sp, H)[:, :S_]
    m = jnp.transpose(m, (0, 1, 4, 2, 3)).reshape(B_, Sp, H)[:, :S_]
    return num, den, m


def _dilated_attention_mixer(h, w_qkv, w_out, rel_bias):
    B_, S_, _ = h.shape
    qkv = (h @ w_qkv).astype(jnp.float32).reshape(B_, S_, 3, N_DIL_GROUPS, ATT_HEADS, HEAD_DIM)
    rb = rel_bias.astype(jnp.float32)
    nums, dens, maxs = [], [], []
    for g, (window, dil) in enumerate(WINDOW_DILATIONS):
        n_, d_, m_ = _dilated_group(qkv[:, :, 0, g], qkv[:, :, 1, g], qkv[:, :, 2, g],
                                    rb[:, g * ATT_HEADS:(g + 1) * ATT_HEADS], window, dil)
        nums.append(n_)
        dens.append(d_)
        maxs.append(m_)
    m_all = jnp.max(jnp.stack(maxs), axis=0)
    scales = [jnp.exp(m_ - m_all) for m_ in maxs]
    num_sum = sum(n_ * c[..., None] for n_, c in zip(nums, scales))
    den_sum = sum(d_ * c for d_, c in zip(dens, scales))
    o = (num_sum / den_sum[..., None]).astype(h.dtype).reshape(B_, S_, ATT_WIDTH)
    return o @ w_out


def setup_inputs(seed: int = 0) -> dict:
    key = jax.random.key(seed)
    ks = jax.random.split(key, 16)
    f32 = jnp.float32
    nrm = lambda k, shape, s: jax.random.normal(k, shape, f32) * s
    qkv_cols = 3 * N_DIL_GROUPS * ATT_HEADS * HEAD_DIM
    return {
        "x": jax.random.normal(ks[0], (BATCH, SEQ, D_MODEL), f32),
        "mix_norm_g": 1.0 + nrm(ks[1], (DEPTH, D_MODEL), 0.05),
        "mlp_norm_g": 1.0 + nrm(ks[2], (DEPTH, D_MODEL), 0.05),
        "final_norm_g": 1.0 + nrm(ks[3], (D_MODEL,), 0.05),
        "a_w_in": nrm(ks[4], (N_A_LAYERS, D_MODEL, 2 * GATE_WIDTH), D_MODEL ** -0.5),
        "a_ln_g": 1.0 + nrm(ks[5], (N_A_LAYERS, GATE_WIDTH), 0.05),
        "a_ln_b": nrm(ks[6], (N_A_LAYERS, GATE_WIDTH), 0.02),
        "a_w_s": nrm(ks[7], (N_A_LAYERS, GATE_GROUPS, CHUNK, CHUNK), CHUNK ** -0.5),
        "a_b_s": 1.0 + nrm(ks[8], (N_A_LAYERS, GATE_GROUPS, CHUNK), 0.1),
        "a_w_out": nrm(ks[9], (N_A_LAYERS, GATE_WIDTH, D_MODEL), GATE_WIDTH ** -0.5),
        "b_w_qkv": nrm(ks[10], (N_B_LAYERS, D_MODEL, qkv_cols), D_MODEL ** -0.5),
        "b_w_out": nrm(ks[11], (N_B_LAYERS, ATT_WIDTH, D_MODEL), ATT_WIDTH ** -0.5),
        "rel_bias": nrm(ks[12], (N_BUCKETS, N_DIL_GROUPS * ATT_HEADS), 0.5),
        "w_up": nrm(ks[13], (DEPTH, D_MODEL, D_FF), D_MODEL ** -0.5),
        "w_down": nrm(ks[14], (DEPTH, D_FF, D_MODEL), D_FF ** -0.5),
    }


def reference(x, mix_norm_g, mlp_norm_g, final_norm_g, a_w_in, a_ln_g, a_ln_b, a_w_s, a_b_s,
              a_w_out, b_w_qkv, b_w_out, rel_bias, w_up, w_down):
    h = x
    for layer in range(DEPTH):
        y = _rms_norm(h, mix_norm_g[layer])
        j = layer // N_MIXERS
        if layer % N_MIXERS == 0:
            y = _chunk_gating_mixer(y, a_w_in[j], a_ln_g[j], a_ln_b[j], a_w_s[j], a_b_s[j], a_w_out[j])
        else:
            y = _dilated_attention_mixer(y, b_w_qkv[j], b_w_out[j], rel_bias)
        h = h + y
        y = _rms_norm(h, mlp_norm_g[layer])
        h = h + jnp.square(jax.nn.relu(y @ w_up[layer])) @ w_down[layer]
    return _rms_norm(h, final_norm_g)
```

```python
import math
from contextlib import ExitStack
import numpy as np
import concourse.bass as bass
import concourse.mybir as mybir
from concourse.bass_utils import run_bass_kernel_spmd

F32 = mybir.dt.float32
BF16 = mybir.dt.bfloat16
AF = mybir.ActivationFunctionType
ALU = mybir.AluOpType

ENGS = ("pe", "act", "dve", "pool", "sp")
D = 1024
NOWN = 4096
NHALO = 2048
NT = NOWN + NHALO
EPS = 1e-6


class Sched:
    def __init__(self, nc, stack):
        self.nc = nc
        self.stack = stack
        self.ops = []
        self.eng_ops = {e: [] for e in ENGS}
        self.last_w = {}
        self.readers = {}
        self.dma_cnt = {}
        self.dma_last = {}
        self.observed = {e: {} for e in ENGS}
        self.barrier_deps = []

    def barrier(self, keep=()):
        kept = {k: self.last_w[k] for k in keep if k in self.last_w}
        deps = []
        for e in ENGS:
            for op in reversed(self.eng_ops[e]):
                if op["dma"] is None:
                    deps.append(op["id"])
                    break
        deps.extend(self.dma_last.values())
        self.barrier_deps = deps
        self.last_w = dict(kept)
        self.readers = {}

    def add(self, eng, fn, reads=(), writes=(), dma=None):
        oid = len(self.ops)
        deps = set(self.barrier_deps)
        for r in reads:
            w = self.last_w.get(r)
            if w is not None:
                deps.add(w)
        for wkey in writes:
            w = self.last_w.get(wkey)
            if w is not None:
                deps.add(w)
            rl = self.readers.get(wkey)
            if rl:
                deps.update(rl)
        op = dict(id=oid, eng=eng, fn=fn, dma=dma, seq=len(self.eng_ops[eng]),
                  waits=[], need_inc=False)
        need = {}
        obs = self.observed[eng]
        for d in deps:
            dop = self.ops[d]
            if dop["dma"] is not None:
                src = ("dma", dop["dma"])
                val = self.dma_cnt[dop["dma"]]
            else:
                if dop["eng"] == "pe" and eng == "pe":
                    continue
                src = ("eng", dop["eng"])
                val = dop["seq"]
            if val <= obs.get(src, -1):
                continue
            if src not in need or need[src][0] < val:
                need[src] = (val, d)
        for src, (val, d) in need.items():
            obs[src] = val
            if src[0] == "dma":
                op["waits"].append((src, val))
            else:
                self.ops[d]["need_inc"] = True
                op["waits"].append((src, d))
        if dma is not None:
            self.dma_cnt[dma] = self.dma_cnt.get(dma, 0) + 16
            op["dma_val"] = self.dma_cnt[dma]
            self.dma_last[dma] = oid
        self.ops.append(op)
        self.eng_ops[eng].append(op)
        for r in reads:
            self.readers.setdefault(r, set()).add(oid)
        for wkey in writes:
            self.last_w[wkey] = oid
            self.readers[wkey] = set()
        return oid

    def emit(self, final_waits=()):
        nc = self.nc
        sems = {}
        for e in ENGS:
            sems[("eng", e)] = self.stack.enter_context(nc.semaphore("s_" + e))
        for k in self.dma_cnt:
            sems[("dma", k)] = self.stack.enter_context(nc.semaphore("d_" + str(k)))
        for e in ENGS:
            c = 0
            for op in self.eng_ops[e]:
                if op["dma"] is None and op["need_inc"]:
                    c += 1
                    op["cnt"] = c
        ops = self.ops

        def run(e, engobj):
            for op in self.eng_ops[e]:
                for src, v in op["waits"]:
                    if src[0] == "dma":
                        engobj.wait_ge(sems[src], v)
                    else:
                        engobj.wait_ge(sems[src], ops[v]["cnt"])
                ins = op["fn"](engobj)
                if op["dma"] is not None:
                    ins.then_inc(sems[("dma", op["dma"])], 16)
                elif op["need_inc"]:
                    ins.then_inc(sems[("eng", e)], 1)
            if e == "sp":
                for k in final_waits:
                    engobj.wait_ge(sems[("dma", k)], self.dma_cnt[k])

        with nc.Block() as block:
            @block.tensor
            def _(eng):
                run("pe", eng)

            @block.scalar
            def _(eng):
                run("act", eng)

            @block.vector
            def _(eng):
                run("dve", eng)

            @block.gpsimd
            def _(eng):
                run("pool", eng)

            @block.sync
            def _(eng):
                run("sp", eng)


class Builder:
    def __init__(self, debug=False, phases=("A", "B0", "C2", "C3", "B1")):
        self.debug = debug
        self.phases = phases
        self.nc = bass.Bass("TRN2", target_bir_lowering=False)
        self.psrr = 0
        self.castrr = 0
        self.uid = 0

    def dram(self, name, shape, dt, kind):
        return self.nc.dram_tensor(name, list(shape), dt, kind=kind).ap()

    def dump(self, name, ap, shape, dt, reads):
        if not self.debug:
            return
        d = self.dram("dbg_" + name, shape, dt, "ExternalOutput")
        self.S.add("sp", lambda e: e.dma_start(out=d, in_=ap), reads=reads, dma="dbg")

    def nextps(self):
        b = self.psrr % 8
        self.psrr += 1
        return b

    def load_w(self, src, dst, wkey):
        S = self.S
        slot = self.castrr % 2
        eng = ("act", "dve")[self.castrr % 2]
        self.castrr += 1
        stg = self.stg[slot]
        n = 1
        for s in src.shape[1:]:
            n *= s
        sv = stg[:, 0:n]
        if len(src.shape) == 3:
            sv = sv.rearrange("p (a b) -> p a b", a=src.shape[1])
        S.add("sp", lambda e: e.dma_start(out=sv, in_=src), writes=[("stg", slot)], dma="stg%d" % slot)
        if eng == "act":
            S.add("act", lambda e: e.copy(out=dst, in_=sv), reads=[("stg", slot)], writes=[wkey])
        else:
            S.add("dve", lambda e: e.tensor_copy(out=dst, in_=sv), reads=[("stg", slot)], writes=[wkey])

    def load_w_split(self, src, dst, wkey):
        S = self.S
        slot = self.castrr % 2
        eng = ("act", "dve")[self.castrr % 2]
        self.castrr += 1
        stg = self.stg[slot]
        n = 1
        for s_ in src.shape[1:]:
            n *= s_
        sv = stg[:, 0:n]
        if len(src.shape) == 3:
            sv = sv.rearrange("p (a b) -> p a b", a=src.shape[1])

        def dma():
            S.add("sp", lambda e: e.dma_start(out=sv, in_=src), writes=[("stg", slot)], dma="stg%d" % slot)

        def cast():
            if eng == "act":
                S.add("act", lambda e: e.copy(out=dst, in_=sv), reads=[("stg", slot)], writes=[wkey])
            else:
                S.add("dve", lambda e: e.tensor_copy(out=dst, in_=sv), reads=[("stg", slot)], writes=[wkey])
        return dma, cast

    def rmsnorm(self, ht, hkey, gvec, T, out, outkey, tagp):
        S = self.S
        b = self.nextps()
        ps = self.ps[b]
        for c in range(8):
            sl = c % 2
            sqv = self.sqc[sl][:, 0:T]
            S.add("act", lambda e, c=c, sqv=sqv: e.activation(out=sqv, in_=ht(c), func=AF.Square),
                  reads=[hkey], writes=[("sqc", sl)])
            S.add("pe", lambda e, c=c, sqv=sqv: e.matmul(ps[:, 0:T], lhsT=self.onesm[:], rhs=sqv,
                                                       start=(c == 0), stop=(c == 7)),
                  reads=[("sqc", sl), "onesm"], writes=[("ps", b)])
        rs = self.rstd[:, 0:T]
        S.add("act", lambda e: e.activation(out=rs, in_=ps[:, 0:T], func=AF.Sqrt, bias=self.epst[:]),
              reads=[("ps", b), "epst"], writes=["rstd"])
        S.add("dve", lambda e: e.reciprocal(out=rs, in_=rs), reads=["rstd"], writes=["rstd"])
        for c in range(8):
            S.add("dve", lambda e, c=c: e.scalar_tensor_tensor(out=out(c), in0=ht(c), scalar=gvec[:, c:c + 1],
                                                               in1=rs, op0=ALU.mult, op1=ALU.mult),
                  reads=[hkey, "rstd", "gvec"], writes=[outkey])

    def build(self):
        nc = self.nc
        dbg = self.debug
        IN, OUT, INT = "ExternalInput", "ExternalOutput", "Internal"
        xT = self.dram("xT", [D, NT], F32, IN)
        w_in = self.dram("a_w_in", [D, 2048], F32, IN)
        w_ao = self.dram("a_w_out", [D, D], F32, IN)
        w_up = self.dram("w_up", [2, D, 4096], F32, IN)
        w_dn = self.dram("w_down", [2, 4096, D], F32, IN)
        gvecs = self.dram("gvecs", [128, 6, 8], F32, IN)
        lngb = self.dram("lngb", [128, 2, D], F32, IN)
        wsT_d = self.dram("wsT", [128, 8, 128], F32, IN)
        tril_d = self.dram("trilm", [128, 128], F32, IN)
        bsb_d = self.dram("bsb", [128, 8, 128], F32, IN)
        w_qkv = self.dram("b_w_qkv", [D, 4608], F32, IN)
        w_bo = self.dram("b_w_out", [512, D], F32, IN)
        bt_d = self.dram("btab", [128, 24, 2, 128], F32, IN)
        bm_d = self.dram("bmask", [128, 2, 128], F32, IN)
        flag_d = self.dram("hflag", [128, 1], F32, IN)
        ident_d = self.dram("ident", [128, 128], F32, IN)
        sk = OUT if dbg else INT
        xnT = self.dram("xnT", [D, NT], BF16, sk)
        acc_d = self.dram("acc", [NOWN, 528], F32, sk)
        h1T = self.dram("h1T", [D, NT], F32, sk)
        h2T = self.dram("h2T", [D, NT], F32, sk)
        h3T = self.dram("h3T", [D, NOWN], F32, sk)
        outT = self.dram("outT", [D, NOWN], F32, OUT)

        with ExitStack() as st:
            S = self.S = Sched(nc, st)
            sb = lambda name, shape, dt: st.enter_context(nc.sbuf_tensor(name, list(shape), dt))
            self.ps = [st.enter_context(nc.psum_tensor("ps%d" % i, [128, 512], F32)) for i in range(8)]
            self.onesm = sb("onesm", [128, 128], F32)
            self.epst = sb("epst", [128, 1], F32)
            self.gv = sb("gv", [128, 6, 8], F32)
            self.stg = [sb("stg%d" % i, [128, 2048], F32) for i in range(2)]
            self.sqc = [sb("sqc%d" % i, [128, 512], F32) for i in range(2)]
            self.rstd = sb("rstd", [128, 512], F32)
            S.add("pool", lambda e: e.memset(self.onesm[:], 1.0 / D), writes=["onesm"])
            S.add("pool", lambda e: e.memset(self.epst[:], EPS), writes=["epst"])
            S.add("sp", lambda e: e.dma_start(out=self.gv[:], in_=gvecs), writes=["gvec"], dma="const")

            if "A" in self.phases:
                with ExitStack() as pst:
                    self.phase_A(pst, xT, w_in, w_ao, lngb, wsT_d, tril_d, bsb_d, h1T)
                S.barrier()
            if "B0" in self.phases:
                with ExitStack() as pst:
                    self.phase_ffn(pst, 0, h1T, h2T, NT, w_up, w_dn, None, xnT=xnT)
                S.barrier()
            if "C1" in self.phases:
                with ExitStack() as pst:
                    self.phase_C1(pst, h2T, xnT)
                S.barrier()
            if "C2" in self.phases:
                with ExitStack() as pst:
                    self.phase_C2(pst, xnT, w_qkv, bt_d, bm_d, flag_d, acc_d, ident_d)
                S.barrier()
            with ExitStack() as wst:
                pre = self.ffn_weights(wst, 1, w_up, w_dn) if "B1" in self.phases else None
                if "C3" in self.phases:
                    with ExitStack() as pst:
                        self.phase_C3(pst, acc_d, h2T, w_bo, ident_d, h3T, pre[2] if pre else None)
                    S.barrier(keep=[("wup", kc) for kc in range(8)] + [("wdn", f2) for f2 in range(16)])
                if "B1" in self.phases:
                    with ExitStack() as pst:
                        self.phase_ffn(pst, 1, h3T, None, NOWN, w_up, w_dn, outT, pre=pre)
                    S.barrier()
            S.emit(final_waits=list(S.dma_cnt.keys()))
        return nc

    def phase_A(self, pst, xT, w_in, w_ao, lngb, wsT_d, tril_d, bsb_d, h1T):
        nc, S = self.nc, self.S
        T = 256
        sb = lambda name, shape, dt: pst.enter_context(nc.sbuf_tensor(name, list(shape), dt))
        win = sb("A_win", [128, 8, 2048], BF16)
        wao = sb("A_wao", [128, 8, D], BF16)
        wsT = sb("A_wsT", [128, 8, 128], BF16)
        wsf = sb("A_wsf", [128, 8, 128], F32)
        trl = sb("A_trl", [128, 128], F32)
        lng = sb("A_lng", [128, 2, D], F32)
        bsb = sb("A_bsb", [128, 8, 128], F32)
        xt = [sb("A_xt%d" % i, [128, 8, T], F32) for i in range(4)]
        xn = [sb("A_xn%d" % i, [128, 8, T], BF16) for i in range(2)]
        uT = [sb("A_uT%d" % i, [128, 8, T], F32) for i in range(2)]
        vtok = [sb("A_vtok%d" % i, [128, D], F32) for i in range(2)]
        vn = [sb("A_vn%d" % i, [128, D], BF16) for i in range(2 * (T // 128))]
        st6 = sb("A_st6", [128, 2, 6], F32)
        mv = sb("A_mv", [128, 2], F32)
        rsv = sb("A_rsv", [128, 1], F32)
        gtmp = [sb("A_gtmp%d" % i, [128, T], F32) for i in range(2)]
        gu = sb("A_gu", [128, 8, T], BF16)

        S.add("sp", lambda e: e.dma_start(out=wsf[:], in_=wsT_d), writes=["wsf"], dma="const")
        S.add("sp", lambda e: e.dma_start(out=trl[:], in_=tril_d), writes=["trl"], dma="const")
        S.add("sp", lambda e: e.dma_start(out=lng[:], in_=lngb), writes=["lng"], dma="const")
        S.add("sp", lambda e: e.dma_start(out=bsb[:], in_=bsb_d), writes=["bsb"], dma="const")
        for g in range(8):
            S.add("pool", lambda e, g=g: e.tensor_tensor(out=wsf[:, g, :], in0=wsf[:, g, :], in1=trl[:], op=ALU.mult),
                  reads=["wsf", "trl"], writes=["wsf"])
        S.add("dve", lambda e: e.tensor_copy(out=wsT[:], in_=wsf[:]), reads=["wsf"], writes=["wsT"])
        for g in range(8):
            b = self.nextps()
            S.add("pe", lambda e, g=g, b=b: e.matmul(self.ps[b][:, 0:128], lhsT=lng[:, 1, g * 128:(g + 1) * 128],
                                                    rhs=wsf[:, g, :], start=True, stop=True),
                  reads=["lng", "wsf"], writes=[("ps", b)])
            S.add("dve", lambda e, g=g, b=b: e.tensor_tensor(out=bsb[:, g, :], in0=self.ps[b][:, 0:128], in1=bsb[:, g, :], op=ALU.add),
                  reads=[("ps", b), "bsb"], writes=["bsb"])
        for kc in range(8):
            self.load_w(w_in[kc * 128:(kc + 1) * 128, :], win[:, kc, :], ("win", kc))
        for kc2 in range(4):
            self.load_w(w_ao[kc2 * 256:(kc2 + 1) * 256, :].rearrange("(c p) n -> p c n", p=128),
                        wao[:, 2 * kc2:2 * kc2 + 2, :], ("wao", kc2))

        ntiles = NT // T
        NJ = T // 128

        def ld(i):
            sl3 = i % 4
            xs = xt[sl3]
            t0 = i * T
            S.add("sp", lambda e, xs=xs, t0=t0: e.dma_start(
                out=xs[:], in_=xT[:, t0:t0 + T].rearrange("(c p) t -> p c t", p=128)),
                writes=[("xt", sl3)], dma="xa%d" % sl3)

        def front(i):
            sl3 = i % 4
            s2 = i % 2
            xs = xt[sl3]
            xnn = xn[s2]
            self.rmsnorm(lambda c, xs=xs: xs[:, c, :], ("xt", sl3), self.gv[:, 0, :], T,
                         lambda c, xnn=xnn: xnn[:, c, :], ("xn", s2), "A")

        def mid(i):
            s2 = i % 2
            xnn = xn[s2]
            uu = uT[s2]
            for f in range(8):
                b = self.nextps()
                ps = self.ps[b]
                for kc in range(8):
                    S.add("pe", lambda e, f=f, kc=kc, ps=ps, xnn=xnn: e.matmul(
                        ps[:, 0:T], lhsT=win[:, kc, f * 128:(f + 1) * 128], rhs=xnn[:, kc, :],
                        start=(kc == 0), stop=(kc == 7)),
                        reads=[("xn", s2), ("win", kc)], writes=[("ps", b)])
                S.add("act", lambda e, f=f, ps=ps, uu=uu: e.activation(out=uu[:, f, :], in_=ps[:, 0:T], func=AF.Gelu),
                      reads=[("ps", b)], writes=[("uT", s2, f)])
            for j in range(NJ):
                vs = (i * NJ + j) % 2
                vt = vtok[vs]
                for hf in range(2):
                    b = self.nextps()
                    ps = self.ps[b]
                    for kc in range(8):
                        S.add("pe", lambda e, j=j, hf=hf, kc=kc, ps=ps, xnn=xnn: e.matmul(
                            ps[:], lhsT=xnn[:, kc, j * 128:(j + 1) * 128],
                            rhs=win[:, kc, 1024 + hf * 512:1024 + (hf + 1) * 512],
                            start=(kc == 0), stop=(kc == 7)),
                            reads=[("xn", s2), ("win", kc)], writes=[("ps", b)])
                    S.add("act", lambda e, hf=hf, ps=ps, vt=vt: e.activation(
                        out=vt[:, hf * 512:(hf + 1) * 512], in_=ps[:], func=AF.Gelu),
                        reads=[("ps", b)], writes=[("vtok", vs)])
                for hf in range(2):
                    S.add("dve", lambda e, hf=hf, vt=vt: e.bn_stats(out=st6[:, hf, :], in_=vt[:, hf * 512:(hf + 1) * 512]),
                          reads=[("vtok", vs)], writes=["st6"])
                S.add("dve", lambda e: e.bn_aggr(out=mv[:], in_=st6[:].rearrange("p a b -> p (a b)")), reads=["st6"], writes=["mv"])
                S.add("act", lambda e: e.activation(out=rsv[:], in_=mv[:, 1:2], func=AF.Sqrt, bias=self.epst[:]),
                      reads=["mv", "epst"], writes=["rsv"])
                S.add("dve", lambda e: e.reciprocal(out=rsv[:], in_=rsv[:]), reads=["rsv"], writes=["rsv"])
                vi = s2 * NJ + j
                vnj = vn[vi]
                S.add("dve", lambda e, vt=vt, vnj=vnj: e.tensor_scalar(out=vnj[:], in0=vt[:], scalar1=mv[:, 0:1], scalar2=rsv[:],
                                                                       op0=ALU.subtract, op1=ALU.mult),
                      reads=[("vtok", vs), "mv", "rsv"], writes=[("vn", vi)])

        def back(i):
            sl3 = i % 4
            s2 = i % 2
            xs = xt[sl3]
            uu = uT[s2]
            t0 = i * T
            for g in range(8):
                b = self.nextps()
                for j in range(NJ):
                    vi = s2 * NJ + j
                    vnj = vn[vi]
                    S.add("pe", lambda e, g=g, j=j, b=b, vnj=vnj: e.matmul(
                        self.ps[b][:, j * 128:(j + 1) * 128], lhsT=vnj[:, g * 128:(g + 1) * 128], rhs=wsT[:, g, :],
                        start=True, stop=True),
                        reads=[("vn", vi), "wsT"], writes=[("ps", b)])
                gs = g % 2
                gt = gtmp[gs]
                for j in range(NJ):
                    S.add("dve", lambda e, g=g, b=b, gt=gt, j=j: e.scalar_tensor_tensor(
                        out=gt[:, j * 128:(j + 1) * 128], in0=self.ps[b][:, j * 128:(j + 1) * 128],
                        scalar=self.gv[:, 5, g:g + 1], in1=bsb[:, g, :], op0=ALU.mult, op1=ALU.add),
                        reads=[("ps", b), "bsb", "gvec"], writes=[("gtmp", gs)])
                S.add("pool", lambda e, g=g, gt=gt, uu=uu: e.tensor_tensor(out=gu[:, g, :], in0=gt[:], in1=uu[:, g, :], op=ALU.mult),
                      reads=[("gtmp", gs), ("uT", s2, g)], writes=[("gu", g)])

        def back_o(i):
            sl3 = i % 4
            xs = xt[sl3]
            t0 = i * T
            for f in range(8):
                b = self.nextps()
                ps = self.ps[b]
                for g in range(8):
                    S.add("pe", lambda e, f=f, g=g, ps=ps: e.matmul(
                        ps[:, 0:T], lhsT=wao[:, g, f * 128:(f + 1) * 128], rhs=gu[:, g, :],
                        start=(g == 0), stop=(g == 7)),
                        reads=[("gu", g), ("wao", g // 2)], writes=[("ps", b)])
                S.add("dve", lambda e, f=f, ps=ps, xs=xs: e.tensor_tensor(out=xs[:, f, :], in0=ps[:, 0:T], in1=xs[:, f, :], op=ALU.add),
                      reads=[("ps", b), ("xt", sl3)], writes=[("xt", sl3)])
            S.add("sp", lambda e, xs=xs, t0=t0: e.dma_start(
                out=h1T[:, t0:t0 + T].rearrange("(c p) t -> p c t", p=128), in_=xs[:]),
                reads=[("xt", sl3)], writes=[("h1T", t0 // 512)], dma="sa%d" % sl3)

        ld(0)
        ld(1)
        front(0)
        for i in range(ntiles):
            if i + 2 < ntiles:
                ld(i + 2)
            if i >= 1:
                back(i - 1)
            if i + 1 < ntiles:
                front(i + 1)
            mid(i)
            if i >= 1:
                back_o(i - 1)
        back(ntiles - 1)
        back_o(ntiles - 1)

    def phase_C1(self, pst, h2T, xnT):
        nc, S = self.nc, self.S
        T = 512
        sb = lambda name, shape, dt: pst.enter_context(nc.sbuf_tensor(name, list(shape), dt))
        ht = [sb("C1_ht%d" % i, [128, 8, T], F32) for i in range(2)]
        xo = [sb("C1_xo%d" % i, [128, 8, T], BF16) for i in range(2)]
        for i in range(NT // T):
            sl = i % 2
            hs, xs = ht[sl], xo[sl]
            t0 = i * T
            S.add("sp", lambda e, hs=hs, t0=t0: e.dma_start(
                out=hs[:], in_=h2T[:, t0:t0 + T].rearrange("(c p) t -> p c t", p=128)),
                reads=[("h2T", i)], writes=[("ht", sl)], dma="xt%d" % sl)
            self.rmsnorm(lambda c, hs=hs: hs[:, c, :], ("ht", sl), self.gv[:, 2, :], T,
                         lambda c, xs=xs: xs[:, c, :], ("xo", sl), "C")
            S.add("sp", lambda e, xs=xs, t0=t0: e.dma_start(
                out=xnT[:, t0:t0 + T].rearrange("(c p) t -> p c t", p=128), in_=xs[:]),
                reads=[("xo", sl)], writes=[("xnT", i)], dma="st%d" % sl)

    def phase_C2(self, pst, xnT, w_qkv, bt_d, bm_d, flag_d, acc_d, ident_d):
        nc, S = self.nc, self.S
        sb = lambda name, shape, dt: pst.enter_context(nc.sbuf_tensor(name, list(shape), dt))
        xn = sb("C_xn", [128, 8, 2048], BF16)
        wsets = [tuple(sb("C_w%s%d" % (nm, i), [128, 8, 512], BF16) for nm in ("q", "k", "v")) for i in range(2)]
        nslots = (2, 2, 5)
        Kq = [[sb("C_K%d_%d" % (g, i), [128, 4, 4, 128], BF16) for i in range(nslots[g])] for g in range(3)]
        Vq = [[sb("C_V%d_%d" % (g, i), [128, 4, 8, 66], BF16) for i in range(nslots[g])] for g in range(3)]
        Qq = [sb("C_Q%d" % i, [128, 4, 2, 4, 128], BF16) for i in range(1)]
        E = sb("C_E", [128, 8, 2, 128], F32)
        bmr = sb("C_bm", [128, 256], F32)
        ktok = sb("C_ktok", [128, 512], BF16)
        qtok = sb("C_qtok", [128, 512], BF16)
        idb = sb("C_idb", [128, 128], BF16)
        flag = sb("C_flag", [128, 1], F32)
        expS = [sb("C_expS%d" % i, [128, 512], F32) for i in range(2)] + [self.sqc[0], self.sqc[1]]
        Pb = [sb("C_P%d" % i, [128, 512], BF16) for i in range(2)]
        accsb = [sb("C_acc%d" % i, [128, 528], F32) for i in range(2)]
        idf = accsb[1][:, 0:128]

        S.add("sp", lambda e: e.dma_start(out=bmr[:], in_=bm_d.rearrange("p b c -> p (b c)")), writes=["bm"], dma="const")
        S.add("sp", lambda e: e.dma_start(out=idf, in_=ident_d), writes=[("accsb", 1)], dma="const")
        S.add("dve", lambda e: e.tensor_copy(out=idb[:], in_=idf), reads=[("accsb", 1)], writes=["idb"])
        S.add("sp", lambda e: e.dma_start(out=flag[:], in_=flag_d), writes=["flag"], dma="const")
        for g in range(3):
            for i in range(nslots[g]):
                S.add("pool", lambda e, g=g, i=i: e.memset(Vq[g][i][:], 1.0), writes=[("V", g, i)])
        for i in range(1):
            S.add("pool", lambda e, i=i: e.memset(Qq[i][:], 0.0), writes=[("Q", i)])

        import os
        lvl = int(os.environ.get("KDBG_C2", "9"))
        cur01 = [0, 0]
        prev01 = [None, None]
        g2slot = [None] * 4
        g2free = [0, 1, 2, 3, 4]
        qrr = 0
        tcount = 0
        dils = (1, 4, 16)
        def group_thunks(k):
            sb_, g_ = k // 3, k % 3
            wq_, wk_, wv_ = wsets[k % 2]
            mats = [(2, wv_, "wv"), (1, wk_, "wk")] + ([(0, wq_, "wq")] if sb_ > 0 else [])
            out = []
            for (qi, wt, wname) in mats:
                c0 = (qi * 3 + g_) * 512
                for k4 in range(2):
                    out.append((w_qkv[k4 * 512:(k4 + 1) * 512, c0:c0 + 512].rearrange("(c p) n -> p c n", p=128),
                                wt[:, 4 * k4:4 * k4 + 4, :], (wname, k % 2, k4)))
            return out

        pend = {"args": [], "casts": []}

        def pf_step():
            if pend["casts"]:
                pend["casts"].pop(0)()
            if pend["args"]:
                d, c = self.load_w_split(*pend["args"].pop(0))
                d()
                pend["casts"].append(c)

        def pf_flush():
            while pend["args"] or pend["casts"]:
                pf_step()

        for sbk in range(3):
            own = sbk > 0
            S.add("sp", lambda e, sbk=sbk: e.dma_start(
                out=xn[:], in_=xnT[:, sbk * 2048:(sbk + 1) * 2048].rearrange("(c p) t -> p c t", p=128)),
                reads=[("xnT", 4 * sbk + j) for j in range(4)], writes=["xnC"], dma="xnC")
            for g in range(3):
                kgrp = sbk * 3 + g
                pf_flush()
                if kgrp == 0:
                    pend["args"] = group_thunks(0)
                    pf_flush()
                if kgrp + 1 < 9:
                    pend["args"] = group_thunks(kgrp + 1)
                wset = kgrp % 2
                wq, wk, wv = wsets[wset]
                if own:
                    S.add("sp", lambda e, g=g: e.dma_start(out=E[:], in_=bt_d[:, g * 8:(g + 1) * 8, :, :]),
                          writes=["E"], dma="Eg")
                    Ef = E[:].rearrange("p a b c -> p (a b c)")
                    S.add("act", lambda e, Ef=Ef: e.activation(out=Ef, in_=Ef, func=AF.Exp), reads=["E"], writes=["E"])
                    S.add("dve", lambda e: e.tensor_tensor(
                        out=E[:].rearrange("p a b c -> p a (b c)"), in0=E[:].rearrange("p a b c -> p a (b c)"),
                        in1=bmr[:].unsqueeze(1).to_broadcast([128, 8, 256]), op=ALU.mult),
                        reads=["E", "bm"], writes=["E"])
                quads = list(range(4)) if (own or g == 2) else [3]
                for q in quads:
                    if g < 2:
                        cs = cur01[g]
                        hs_ = prev01[g]
                        cur01[g] = 1 - cs
                        prev01[g] = cs
                    else:
                        cs = g2free.pop(0)
                        hs_ = g2slot[q]
                        if hs_ is not None:
                            g2free.append(hs_)
                        g2slot[q] = cs
                    Kc, Vc = Kq[g][cs], Vq[g][cs]

                    def xq(kc, g=g, q=q):
                        if g == 0:
                            return xn[:, kc, 512 * q:512 * q + 512].rearrange("p (t i) -> p t i", t=4)
                        if g == 1:
                            return xn[:, kc, 512 * q:512 * q + 512].rearrange("p (i r) -> p r i", r=4)
                        return xn[:, kc, :].rearrange("p (i r) -> p r i", r=16)[:, 4 * q:4 * q + 4, :]

                    if own:
                        qs = 0
                        Qc = Qq[qs]
                    for tt in range(4):
                        b = 4 + self.nextps() % 4
                        ps = self.ps[b]
                        for kc in range(8):
                            S.add("pe", lambda e, tt=tt, kc=kc, ps=ps, xq=xq, wv=wv: e.matmul(
                                ps[:], lhsT=xq(kc)[:, tt, :], rhs=wv[:, kc, :],
                                start=(kc == 0), stop=(kc == 7)),
                                reads=["xnC", ("wv", wset, kc // 4)], writes=[("ps", b)])
                        S.add("dve", lambda e, tt=tt, ps=ps, Vc=Vc: e.tensor_copy(
                            out=Vc[:, tt, :, 0:64], in_=ps[:].rearrange("p (h c) -> p h c", c=64)),
                            reads=[("ps", b)], writes=[("V", g, cs)])
                        bk = 4 + self.nextps() % 4
                        psk = self.ps[bk]
                        for kc in range(8):
                            S.add("pe", lambda e, tt=tt, kc=kc, psk=psk, xq=xq, wk=wk: e.matmul(
                                psk[:], lhsT=xq(kc)[:, tt, :], rhs=wk[:, kc, :],
                                start=(kc == 0), stop=(kc == 7)),
                                reads=["xnC", ("wk", wset, kc // 4)], writes=[("ps", bk)])
                        S.add("dve", lambda e, psk=psk: e.tensor_copy(out=ktok[:], in_=psk[:]),
                              reads=[("ps", bk)], writes=["ktok"])
                        if own:
                            bq = 4 + self.nextps() % 4
                            psq = self.ps[bq]
                            for kc in range(8):
                                S.add("pe", lambda e, tt=tt, kc=kc, psq=psq, xq=xq, wq=wq: e.matmul(
                                    psq[:], lhsT=xq(kc)[:, tt, :], rhs=wq[:, kc, :],
                                    start=(kc == 0), stop=(kc == 7)),
                                    reads=["xnC", ("wq", wset, kc // 4)], writes=[("ps", bq)])
                            S.add("act", lambda e, psq=psq: e.mul(out=qtok[:], in_=psq[:], mul=0.125),
                                  reads=[("ps", bq)], writes=["qtok"])
                        pf_step()
                        bt = 4 + self.nextps() % 4
                        pst_ = self.ps[bt]
                        for c in range(4):
                            S.add("pe", lambda e, c=c, pst_=pst_: e.matmul(
                                pst_[:, c * 128:(c + 1) * 128], lhsT=ktok[:, c * 128:(c + 1) * 128], rhs=idb[:],
                                start=True, stop=True),
                                reads=["ktok", "idb"], writes=[("ps", bt)])
                        S.add("act", lambda e, tt=tt, pst_=pst_, Kc=Kc: e.copy(
                            out=Kc[:, :, tt, :], in_=pst_[:].rearrange("p (c i) -> p c i", c=4)),
                            reads=[("ps", bt)], writes=[("K", g, cs)])
                        if own:
                            bt2 = 4 + self.nextps() % 4
                            pst2 = self.ps[bt2]
                            for c in range(4):
                                S.add("pe", lambda e, c=c, pst2=pst2: e.matmul(
                                    pst2[:, c * 128:(c + 1) * 128], lhsT=qtok[:, c * 128:(c + 1) * 128], rhs=idb[:],
                                    start=True, stop=True),
                                    reads=["qtok", "idb"], writes=[("ps", bt2)])
                            for e_ in range(2):
                                pr = slice(64 * e_, 64 * e_ + 64)
                                eng = "act" if e_ == 0 else "dve"
                                if e_ == 0:
                                    S.add("act", lambda e, tt=tt, pst2=pst2, Qc=Qc, pr=pr, e_=e_: e.copy(
                                        out=Qc[pr, :, e_, tt, :], in_=pst2[pr, :].rearrange("p (c i) -> p c i", c=4)),
                                        reads=[("ps", bt2)], writes=[("Q", qs)])
                                else:
                                    S.add("dve", lambda e, tt=tt, pst2=pst2, Qc=Qc, pr=pr, e_=e_: e.tensor_copy(
                                        out=Qc[pr, :, e_, tt, :], in_=pst2[pr, :].rearrange("p (c i) -> p c i", c=4)),
                                        reads=[("ps", bt2)], writes=[("Q", qs)])
                    if not own or lvl < 2:
                        continue
                    def tile_hist(tt):
                        if g == 0:
                            if tt > 0:
                                return Kc, Vc, tt - 1, cs
                            return Kq[g][hs_], Vq[g][hs_], 3, hs_
                        return Kq[g][hs_], Vq[g][hs_], tt, hs_

                    def emit_S(tt):
                        hK, hV, htile, hkey = tile_hist(tt)
                        for hp in range(4):
                            b = 4 + hp
                            ps = self.ps[b]
                            for e_ in range(2):
                                for blk in range(2):
                                    Ksrc = (hK[:, hp, htile, :] if blk == 0 else Kc[:, hp, tt, :])
                                    col = (e_ * 2 + blk) * 128
                                    S.add("pe", lambda e, Ksrc=Ksrc, e_=e_, hp=hp, tt=tt, ps=ps, col=col, Qc=Qc: e.matmul(
                                        ps[:, col:col + 128], lhsT=Ksrc, rhs=Qc[:, hp, e_, tt, :], start=True, stop=True),
                                        reads=[("K", g, hkey), ("K", g, cs), ("Q", qs)], writes=[("ps", b)])

                    emit_S(0)
                    for tt in range(4):
                        if g == 0:
                            if tt > 0:
                                hK, hV, htile, hkey = Kc, Vc, tt - 1, cs
                            else:
                                hK, hV, htile, hkey = Kq[g][hs_], Vq[g][hs_], 3, hs_
                            hal = (sbk == 1 and q == 0 and tt == 0)
                        else:
                            hK, hV, htile, hkey = Kq[g][hs_], Vq[g][hs_], tt, hs_
                            hal = (sbk == 1 and (g == 2 or q == 0))
                        if os.environ.get("KDBG_NOHAL"):
                            hal = False
                        pvb = (tcount % 2) * 2
                        asl = tcount % 2
                        tcount += 1
                        psA, psB = self.ps[pvb], self.ps[pvb + 1]
                        exs = []
                        for hp in range(4):
                            b = 4 + hp
                            ex = expS[hp]
                            S.add("act", lambda e, b=b, ex=ex: e.activation(out=ex[:], in_=self.ps[b][:], func=AF.Exp),
                                  reads=[("ps", b)], writes=[("expS", hp)])
                            exs.append(ex)
                        if tt + 1 < 4:
                            emit_S(tt + 1)
                        for hp in range(4):
                            b = 4 + hp
                            ps = self.ps[b]
                            es = hp % 2
                            ex, pb = exs[hp], Pb[es]
                            gh = 2 * hp
                            Ev = E[:, gh:gh + 2, :, :]
                            meng = "dve"
                            if not hal:
                                S.add(meng, lambda e, ex=ex, pb=pb, Ev=Ev: e.tensor_tensor(
                                    out=pb[:], in0=ex[:], in1=Ev.rearrange("p a b c -> p (a b c)"), op=ALU.mult),
                                    reads=[("expS", hp), "E"], writes=[("P", es)])
                            else:
                                for e2 in range(2):
                                    c_h = e2 * 256
                                    c_c = e2 * 256 + 128
                                    S.add("dve", lambda e, ex=ex, pb=pb, c_c=c_c, gh=gh, e2=e2: e.tensor_tensor(
                                        out=pb[:, c_c:c_c + 128], in0=ex[:, c_c:c_c + 128], in1=E[:, gh + e2, 1, :], op=ALU.mult),
                                        reads=[("expS", hp), "E"], writes=[("P", es)])
                                    S.add("dve", lambda e, ex=ex, pb=pb, c_h=c_h, gh=gh, e2=e2: e.scalar_tensor_tensor(
                                        out=pb[:, c_h:c_h + 128], in0=ex[:, c_h:c_h + 128], scalar=flag[:, 0:1],
                                        in1=E[:, gh + e2, 0, :], op0=ALU.mult, op1=ALU.mult),
                                        reads=[("expS", hp), "E", "flag"], writes=[("P", es)])
                            for e_ in range(2 if lvl >= 3 else 0):
                                h = 2 * hp + e_
                                pso = psA if h < 4 else psB
                                oc = (h % 4) * 66
                                for blk in range(2):
                                    Vsrc = (hV[:, htile, h, :] if blk == 0 else Vc[:, tt, h, :])
                                    pcol = (e_ * 2 + blk) * 128
                                    S.add("pe", lambda e, pso=pso, oc=oc, pb=pb, pcol=pcol, Vsrc=Vsrc, blk=blk: e.matmul(
                                        pso[:, oc:oc + 66], lhsT=pb[:, pcol:pcol + 128], rhs=Vsrc,
                                        start=(blk == 0), stop=(blk == 1)),
                                        reads=[("P", es), ("V", g, hkey), ("V", g, cs)],
                                        writes=[("ps", pvb if h < 4 else pvb + 1)])
                        if lvl < 3:
                            continue
                        ac = accsb[asl]
                        S.add("act", lambda e, ac=ac, psA=psA: e.copy(out=ac[:, 0:264], in_=psA[:, 0:264]),
                              reads=[("ps", pvb)], writes=[("accsb", asl)])
                        S.add("act", lambda e, ac=ac, psB=psB: e.copy(out=ac[:, 264:528], in_=psB[:, 0:264]),
                              reads=[("ps", pvb + 1)], writes=[("accsb", asl)])
                        so = sbk - 1
                        if g == 0:
                            r0 = so * 2048 + 512 * q + 128 * tt
                            dst = acc_d[r0:r0 + 128, :]
                            S.add("sp", lambda e, ac=ac, dst=dst: e.dma_start(out=dst, in_=ac[:]),
                                  reads=[("accsb", asl)], writes=["accd"], dma="acc%d" % asl)
                        else:
                            if g == 1:
                                dst = acc_d.rearrange("(n i r) c -> n r i c", i=128, r=4)[so * 4 + q, tt]
                            else:
                                dst = acc_d.rearrange("(n i r) c -> n r i c", i=128, r=16)[so, 4 * q + tt]
                            S.add("pool", lambda e, ac=ac, dst=dst: e.dma_start(out=dst, in_=ac[:], accum_op=ALU.add),
                                  reads=[("accsb", asl)], writes=["accd"], dma="accp%d" % asl)

    def phase_C3(self, pst, acc_d, h2T, w_bo, ident_d, h3T, wth=None):
        nc, S = self.nc, self.S
        T = 256
        sb = lambda name, shape, dt: pst.enter_context(nc.sbuf_tensor(name, list(shape), dt))
        wo = sb("M_wo", [128, 4, D], BF16)
        idf = sb("M_idf", [128, 128], F32)
        idb = sb("M_idb", [128, 128], BF16)
        a = [[sb("M_a%d_%d" % (g, i), [128, 528], F32) for i in range(4)] for g in range(1)]
        rdens = [sb("M_rden%d" % i, [128, 8], F32) for i in range(2)]
        otok = [sb("M_otok%d" % i, [128, 512], BF16) for i in range(3)]
        oT = sb("M_oT", [128, 4, T], BF16)
        ht = [sb("M_ht%d" % i, [128, 8, T], F32) for i in range(2)]
        S.add("sp", lambda e: e.dma_start(out=idf[:], in_=ident_d), writes=["idf"], dma="const")
        S.add("pool", lambda e: e.tensor_copy(out=idb[:], in_=idf[:]), reads=["idf"], writes=["idb"])
        for k2 in range(2):
            self.load_w(w_bo[k2 * 256:(k2 + 1) * 256, :].rearrange("(c p) n -> p c n", p=128),
                        wo[:, 2 * k2:2 * k2 + 2, :], ("wo", k2))
        NJ = T // 128
        ntl = NOWN // T

        def loads(i):
            sl = i % 2
            t0 = i * T
            S.add("sp", lambda e, sl=sl, t0=t0: e.dma_start(
                out=ht[sl][:], in_=h2T[:, NHALO + t0:NHALO + t0 + T].rearrange("(c p) t -> p c t", p=128)),
                writes=[("ht", sl)], dma="xt%d" % sl)
            for j in range(NJ):
                asl = (i * NJ + j) % 4
                r0 = t0 + 128 * j
                S.add("sp", lambda e, asl=asl, r0=r0: e.dma_start(out=a[0][asl][:], in_=acc_d[r0:r0 + 128, :]),
                      reads=["accd"], writes=[("a", 0, asl)], dma="a0_%d" % asl)

        loads(0)
        for i in range(ntl):
            sl = i % 2
            hs = ht[sl]
            t0 = i * T
            if i + 1 < ntl:
                loads(i + 1)
            for j in range(NJ):
                cnt = i * NJ + j
                asl = cnt % 4
                rden = rdens[cnt % 2]
                rkey = ("rden", cnt % 2)
                if wth and i >= 1:
                    for _ in range(2):
                        if wth:
                            wth.pop(0)()
                a0 = a[0][asl]
                av = a0[:].rearrange("p (h c) -> p h c", c=66)
                S.add("dve", lambda e, av=av, rden=rden: e.reciprocal(out=rden[:], in_=av[:, :, 64]),
                      reads=[("a", 0, asl)], writes=[rkey])
                ok = otok[cnt % 3]
                okey = ("otok", cnt % 3)
                S.add("dve", lambda e, av=av, ok=ok, rden=rden: e.tensor_tensor(
                    out=ok[:].rearrange("p (h c) -> p h c", c=64), in0=av[:, :, 0:64],
                    in1=rden[:].unsqueeze(2).to_broadcast([128, 8, 64]), op=ALU.mult),
                    reads=[("a", 0, asl), rkey], writes=[okey])
                b = self.nextps()
                ps = self.ps[b]
                for c in range(4):
                    S.add("pe", lambda e, c=c, ps=ps, ok=ok: e.matmul(
                        ps[:, c * 128:(c + 1) * 128], lhsT=ok[:, c * 128:(c + 1) * 128], rhs=idb[:],
                        start=True, stop=True),
                        reads=[okey, "idb"], writes=[("ps", b)])
                S.add("act", lambda e, ps=ps, j=j: e.copy(
                    out=oT[:, :, j * 128:(j + 1) * 128], in_=ps[:].rearrange("p (c t) -> p c t", c=4)),
                    reads=[("ps", b)], writes=[("oT", j)])
            for f in range(8):
                b = self.nextps()
                ps = self.ps[b]
                for c in range(4):
                    S.add("pe", lambda e, f=f, c=c, ps=ps: e.matmul(
                        ps[:, 0:T], lhsT=wo[:, c, f * 128:(f + 1) * 128], rhs=oT[:, c, :],
                        start=(c == 0), stop=(c == 3)),
                        reads=[("oT", j) for j in range(NJ)] + [("wo", c // 2)], writes=[("ps", b)])
                S.add("dve", lambda e, f=f, ps=ps, hs=hs: e.tensor_tensor(out=hs[:, f, :], in0=ps[:, 0:T], in1=hs[:, f, :], op=ALU.add),
                      reads=[("ps", b), ("ht", sl)], writes=[("ht", sl)])
            S.add("sp", lambda e, hs=hs, t0=t0: e.dma_start(
                out=h3T[:, t0:t0 + T].rearrange("(c p) t -> p c t", p=128), in_=hs[:]),
                reads=[("ht", sl)], writes=[("h3T", t0 // 512)], dma="st%d" % sl)

    def ffn_weights(self, pst, l, w_up, w_dn):
        nc = self.nc
        P = "F%d_" % l
        wup = pst.enter_context(nc.sbuf_tensor(P + "wup", [128, 8, 4096], BF16))
        wdn = pst.enter_context(nc.sbuf_tensor(P + "wdn", [128, 32, D], BF16))
        th = []
        for kc in range(8):
            for hf in range(2):
                th.append(lambda kc=kc, hf=hf: self.load_w(
                    w_up[l, kc * 128:(kc + 1) * 128, hf * 2048:(hf + 1) * 2048],
                    wup[:, kc, hf * 2048:(hf + 1) * 2048], ("wup", kc)))
        for f2 in range(16):
            th.append(lambda f2=f2: self.load_w(
                w_dn[l, f2 * 256:(f2 + 1) * 256, :].rearrange("(c p) n -> p c n", p=128),
                wdn[:, 2 * f2:2 * f2 + 2, :], ("wdn", f2)))
        return wup, wdn, th

    def phase_ffn(self, pst, l, hin, hout, ntok, w_up, w_dn, outT, xnT=None, pre=None):
        nc, S = self.nc, self.S
        T = 256
        P = "F%d_" % l
        sb = lambda name, shape, dt: pst.enter_context(nc.sbuf_tensor(P + name, list(shape), dt))
        if pre is None:
            wup, wdn, wth = self.ffn_weights(pst, l, w_up, w_dn)
        else:
            wup, wdn, wth = pre
        ht = [sb("ht%d" % i, [128, 8, T], F32) for i in range(2)]
        xn = [sb("xn%d" % i, [128, 8, T], BF16) for i in range(2)]
        hid = sb("hid", [128, 32, T], BF16)
        rl = [sb("rl%d" % i, [128, T], F32) for i in range(4)]
        ot = [sb("ot%d" % i, [128, 8, T], F32) for i in range(1)] if outT is not None else None
        xo = [sb("xo%d" % i, [128, 8, T], BF16) for i in range(2)] if xnT is not None else None

        while wth and len(wth) > 16:
            wth.pop(0)()
        inkey = "h1T" if l == 0 else "h3T"
        ntiles = ntok // T

        def front(i):
            sl = i % 2
            hs = ht[sl]
            t0 = i * T
            S.add("sp", lambda e, hs=hs, t0=t0: e.dma_start(
                out=hs[:], in_=hin[:, t0:t0 + T].rearrange("(c p) t -> p c t", p=128)),
                reads=[(inkey, t0 // 512)], writes=[("ht", sl)], dma="xt%d" % sl)
            self.rmsnorm(lambda c, hs=hs: hs[:, c, :], ("ht", sl), self.gv[:, 1 + 2 * l, :], T,
                         lambda c, sl=sl: xn[sl][:, c, :], ("xn", sl), "F")

        def up(i):
            sl = i % 2
            for f in range(32):
                b = self.nextps()
                ps = self.ps[b]
                for kc in range(8):
                    S.add("pe", lambda e, f=f, kc=kc, ps=ps, sl=sl: e.matmul(
                        ps[:, 0:T], lhsT=wup[:, kc, f * 128:(f + 1) * 128], rhs=xn[sl][:, kc, :],
                        start=(kc == 0), stop=(kc == 7)),
                        reads=[("xn", sl), ("wup", kc)], writes=[("ps", b)])
                rs = f % 4
                r = rl[rs]
                S.add("act", lambda e, ps=ps, r=r: e.activation(out=r[:], in_=ps[:, 0:T], func=AF.Relu),
                      reads=[("ps", b)], writes=[("rl", rs)])
                eng = "pool" if f % 2 == 0 else "dve"
                S.add(eng, lambda e, f=f, r=r: e.tensor_tensor(out=hid[:, f, :], in0=r[:], in1=r[:], op=ALU.mult),
                      reads=[("rl", rs)], writes=[("hid", f)])

        def down(i):
            sl = i % 2
            hs = ht[sl]
            t0 = i * T
            for o in range(8):
                b = self.nextps()
                ps = self.ps[b]
                for f in range(32):
                    S.add("pe", lambda e, f=f, o=o, ps=ps: e.matmul(
                        ps[:, 0:T], lhsT=wdn[:, f, o * 128:(o + 1) * 128], rhs=hid[:, f, :],
                        start=(f == 0), stop=(f == 31)),
                        reads=[("hid", f), ("wdn", f // 2)], writes=[("ps", b)])
                S.add("dve", lambda e, o=o, ps=ps, hs=hs: e.tensor_tensor(out=hs[:, o, :], in0=ps[:, 0:T], in1=hs[:, o, :], op=ALU.add),
                      reads=[("ps", b), ("ht", sl)], writes=[("ht", sl)])
            if hout is not None:
                S.add("sp", lambda e, hs=hs, t0=t0: e.dma_start(
                    out=hout[:, t0:t0 + T].rearrange("(c p) t -> p c t", p=128), in_=hs[:]),
                    reads=[("ht", sl)], writes=[("h2T", t0 // 512)], dma="st%d" % sl)
            if xnT is not None:
                xs = xo[sl]
                self.rmsnorm(lambda c, hs=hs: hs[:, c, :], ("ht", sl), self.gv[:, 2, :], T,
                             lambda c, xs=xs: xs[:, c, :], ("xo", sl), "X")
                S.add("sp", lambda e, xs=xs, t0=t0: e.dma_start(
                    out=xnT[:, t0:t0 + T].rearrange("(c p) t -> p c t", p=128), in_=xs[:]),
                    reads=[("xo", sl)], writes=[("xnT", t0 // 512)], dma="sx%d" % sl)
            if outT is not None:
                os_ = ot[0]
                self.rmsnorm(lambda c, hs=hs: hs[:, c, :], ("ht", sl), self.gv[:, 4, :], T,
                             lambda c, os_=os_: os_[:, c, :], ("ot", 0), "O")
                S.add("sp", lambda e, os_=os_, t0=t0: e.dma_start(
                    out=outT[:, t0:t0 + T].rearrange("(c p) t -> p c t", p=128), in_=os_[:]),
                    reads=[("ot", 0)], dma="so")

        front(0)
        while wth:
            wth.pop(0)()
        for i in range(ntiles):
            up(i)
            if i + 1 < ntiles:
                front(i + 1)
            down(i)


def host_common(inp):
    f = lambda a: np.ascontiguousarray(np.asarray(a, dtype=np.float32))
    gv = np.stack([inp["mix_norm_g"][0], inp["mlp_norm_g"][0], inp["mix_norm_g"][1],
                   inp["mlp_norm_g"][1], inp["final_norm_g"], inp["a_ln_g"][0]], axis=0)
    gv = gv.reshape(6, 8, 128).transpose(2, 0, 1)
    lngb = np.stack([np.broadcast_to(inp["a_ln_g"][0], (128, D)),
                     np.broadcast_to(inp["a_ln_b"][0], (128, D))], axis=1)
    wsT = np.transpose(inp["a_w_s"][0], (2, 0, 1))
    tril = np.triu(np.ones((128, 128), np.float32))
    bsb = np.broadcast_to(inp["a_b_s"][0][None], (128, 8, 128))
    ii = np.arange(128)[None, :]
    kk = np.arange(128)[:, None]
    rel = np.stack([128 + ii - kk, ii - kk], axis=1)
    bmask = ((rel >= 0) & (rel <= 128)).astype(np.float32)
    rb = np.asarray(inp["rel_bias"], np.float32)
    bt = np.empty((128, 24, 2, 128), np.float32)
    for g, dil in enumerate((1, 4, 16)):
        dist = np.clip(rel, 0, 128) * dil
        nf = np.maximum(dist, 1).astype(np.float32)
        large = 16 + (np.log(nf / np.float32(16)) / np.float32(math.log(2048 / 16)) * np.float32(16)).astype(np.int32)
        large = np.minimum(large, 31)
        bucket = np.where(dist < 16, dist, large)
        bt[:, g * 8:(g + 1) * 8] = np.transpose(rb[bucket][..., g * 8:(g + 1) * 8], (0, 3, 1, 2))
    return {
        "b_w_qkv": f(inp["b_w_qkv"][0]), "b_w_out": f(inp["b_w_out"][0]),
        "btab": f(bt), "bmask": f(bmask), "ident": np.eye(128, dtype=np.float32),
        "a_w_in": f(inp["a_w_in"][0]), "a_w_out": f(inp["a_w_out"][0]),
        "w_up": f(inp["w_up"]), "w_down": f(inp["w_down"]),
        "gvecs": f(gv), "lngb": f(lngb), "wsT": f(wsT), "trilm": f(tril), "bsb": f(bsb),
    }


def host_core(x, c):
    b, half = c // 2, c % 2
    xt = np.zeros((D, NT), np.float32)
    s0 = half * NOWN
    xt[:, NHALO:] = x[b, s0:s0 + NOWN].T
    if half == 1:
        xt[:, :NHALO] = x[b, s0 - NHALO:s0].T
    return {"xT": xt, "hflag": np.full((128, 1), float(half), np.float32)}


_NC_CACHE = {}


def kernel(**inputs):
    inp = {k: np.asarray(v) for k, v in inputs.items()}
    if "nc" not in _NC_CACHE:
        _NC_CACHE["nc"] = Builder().build()
    nc = _NC_CACHE["nc"]
    common = host_common(inp)
    x = np.asarray(inp["x"], np.float32)
    in_maps = []
    for c in range(8):
        m = dict(common)
        m.update(host_core(x, c))
        in_maps.append(m)
    res = run_bass_kernel_spmd(nc, in_maps, core_ids=list(range(8)))
    out = np.empty((4, 8192, D), np.float32)
    for c in range(8):
        b, half = c // 2, c % 2
        out[b, half * NOWN:(half + 1) * NOWN] = res.results[c]["outT"].T
    return out
```

```python
import math
from contextlib import ExitStack
import numpy as np
import concourse.bass as bass
import concourse.mybir as mybir
from concourse.bass_utils import run_bass_kernel_spmd

F32 = mybir.dt.float32
BF16 = mybir.dt.bfloat16
AF = mybir.ActivationFunctionType
ALU = mybir.AluOpType

ENGS = ("pe", "act", "dve", "pool", "sp")
D = 1024
NOWN = 4096
NHALO = 2048
NT = NOWN + NHALO
EPS = 1e-6


class Sched:
    def __init__(self, nc, stack):
        self.nc = nc
        self.stack = stack
        self.ops = []
        self.eng_ops = {e: [] for e in ENGS}
        self.last_w = {}
        self.readers = {}
        self.dma_cnt = {}
        self.dma_last = {}
        self.observed = {e: {} for e in ENGS}
        self.barrier_deps = []

    def barrier(self, keep=()):
        kept = {k: self.last_w[k] for k in keep if k in self.last_w}
        deps = []
        for e in ENGS:
            for op in reversed(self.eng_ops[e]):
                if op["dma"] is None:
                    deps.append(op["id"])
                    break
        deps.extend(self.dma_last.values())
        self.barrier_deps = deps
        self.last_w = dict(kept)
        self.readers = {}

    def add(self, eng, fn, reads=(), writes=(), dma=None):
        oid = len(self.ops)
        deps = set(self.barrier_deps)
        for r in reads:
            w = self.last_w.get(r)
            if w is not None:
                deps.add(w)
        for wkey in writes:
            w = self.last_w.get(wkey)
            if w is not None:
                deps.add(w)
            rl = self.readers.get(wkey)
            if rl:
                deps.update(rl)
        op = dict(id=oid, eng=eng, fn=fn, dma=dma, seq=len(self.eng_ops[eng]),
                  waits=[], need_inc=False)
        need = {}
        obs = self.observed[eng]
        for d in deps:
            dop = self.ops[d]
            if dop["dma"] is not None:
                src = ("dma", dop["dma"])
                val = self.dma_cnt[dop["dma"]]
            else:
                if dop["eng"] == "pe" and eng == "pe":
                    continue
                src = ("eng", dop["eng"])
                val = dop["seq"]
            if val <= obs.get(src, -1):
                continue
            if src not in need or need[src][0] < val:
                need[src] = (val, d)
        for src, (val, d) in need.items():
            obs[src] = val
            if src[0] == "dma":
                op["waits"].append((src, val))
            else:
                self.ops[d]["need_inc"] = True
                op["waits"].append((src, d))
        if dma is not None:
            self.dma_cnt[dma] = self.dma_cnt.get(dma, 0) + 16
            op["dma_val"] = self.dma_cnt[dma]
            self.dma_last[dma] = oid
        self.ops.append(op)
        self.eng_ops[eng].append(op)
        for r in reads:
            self.readers.setdefault(r, set()).add(oid)
        for wkey in writes:
            self.last_w[wkey] = oid
            self.readers[wkey] = set()
        return oid

    def emit(self, final_waits=()):
        nc = self.nc
        sems = {}
        for e in ENGS:
            sems[("eng", e)] = self.stack.enter_context(nc.semaphore("s_" + e))
        for k in self.dma_cnt:
            sems[("dma", k)] = self.stack.enter_context(nc.semaphore("d_" + str(k)))
        for e in ENGS:
            c = 0
            for op in self.eng_ops[e]:
                if op["dma"] is None and op["need_inc"]:
                    c += 1
                    op["cnt"] = c
        ops = self.ops

        def run(e, engobj):
            for op in self.eng_ops[e]:
                for src, v in op["waits"]:
                    if src[0] == "dma":
                        engobj.wait_ge(sems[src], v)
                    else:
                        engobj.wait_ge(sems[src], ops[v]["cnt"])
                ins = op["fn"](engobj)
                if op["dma"] is not None:
                    ins.then_inc(sems[("dma", op["dma"])], 16)
                elif op["need_inc"]:
                    ins.then_inc(sems[("eng", e)], 1)
            if e == "sp":
                for k in final_waits:
                    engobj.wait_ge(sems[("dma", k)], self.dma_cnt[k])

        with nc.Block() as block:
            @block.tensor
            def _(eng):
                run("pe", eng)

            @block.scalar
            def _(eng):
                run("act", eng)

            @block.vector
            def _(eng):
                run("dve", eng)

            @block.gpsimd
            def _(eng):
                run("pool", eng)

            @block.sync
            def _(eng):
                run("sp", eng)


class Builder:
    def __init__(self, debug=False, phases=("A", "B0", "C2", "C3", "B1")):
        self.debug = debug
        self.phases = phases
        self.nc = bass.Bass("TRN2", target_bir_lowering=False)
        self.psrr = 0
        self.castrr = 0
        self.uid = 0

    def dram(self, name, shape, dt, kind):
        return self.nc.dram_tensor(name, list(shape), dt, kind=kind).ap()

    def dump(self, name, ap, shape, dt, reads):
        if not self.debug:
            return
        d = self.dram("dbg_" + name, shape, dt, "ExternalOutput")
        self.S.add("sp", lambda e: e.dma_start(out=d, in_=ap), reads=reads, dma="dbg")

    def nextps(self):
        b = self.psrr % 8
        self.psrr += 1
        return b

    def load_w(self, src, dst, wkey):
        S = self.S
        slot = self.castrr % 2
        eng = ("act", "dve")[self.castrr % 2]
        self.castrr += 1
        stg = self.stg[slot]
        n = 1
        for s in src.shape[1:]:
            n *= s
        sv = stg[:, 0:n]
        if len(src.shape) == 3:
            sv = sv.rearrange("p (a b) -> p a b", a=src.shape[1])
        S.add("sp", lambda e: e.dma_start(out=sv, in_=src), writes=[("stg", slot)], dma="stg%d" % slot)
        if eng == "act":
            S.add("act", lambda e: e.copy(out=dst, in_=sv), reads=[("stg", slot)], writes=[wkey])
        else:
            S.add("dve", lambda e: e.tensor_copy(out=dst, in_=sv), reads=[("stg", slot)], writes=[wkey])

    def load_w_split(self, src, dst, wkey):
        S = self.S
        slot = self.castrr % 2
        eng = ("act", "dve")[self.castrr % 2]
        self.castrr += 1
        stg = self.stg[slot]
        n = 1
        for s_ in src.shape[1:]:
            n *= s_
        sv = stg[:, 0:n]
        if len(src.shape) == 3:
            sv = sv.rearrange("p (a b) -> p a b", a=src.shape[1])

        def dma():
            S.add("sp", lambda e: e.dma_start(out=sv, in_=src), writes=[("stg", slot)], dma="stg%d" % slot)

        def cast():
            if eng == "act":
                S.add("act", lambda e: e.copy(out=dst, in_=sv), reads=[("stg", slot)], writes=[wkey])
            else:
                S.add("dve", lambda e: e.tensor_copy(out=dst, in_=sv), reads=[("stg", slot)], writes=[wkey])
        return dma, cast

    def rmsnorm(self, ht, hkey, gvec, T, out, outkey, tagp):
        S = self.S
        b = self.nextps()
        ps = self.ps[b]
        for c in range(8):
            sl = c % 2
            sqv = self.sqc[sl][:, 0:T]
            S.add("act", lambda e, c=c, sqv=sqv: e.activation(out=sqv, in_=ht(c), func=AF.Square),
                  reads=[hkey], writes=[("sqc", sl)])
            S.add("pe", lambda e, c=c, sqv=sqv: e.matmul(ps[:, 0:T], lhsT=self.onesm[:], rhs=sqv,
                                                       start=(c == 0), stop=(c == 7)),
                  reads=[("sqc", sl), "onesm"], writes=[("ps", b)])
        rs = self.rstd[:, 0:T]
        S.add("act", lambda e: e.activation(out=rs, in_=ps[:, 0:T], func=AF.Sqrt, bias=self.epst[:]),
              reads=[("ps", b), "epst"], writes=["rstd"])
        S.add("dve", lambda e: e.reciprocal(out=rs, in_=rs), reads=["rstd"], writes=["rstd"])
        for c in range(8):
            S.add("dve", lambda e, c=c: e.scalar_tensor_tensor(out=out(c), in0=ht(c), scalar=gvec[:, c:c + 1],
                                                               in1=rs, op0=ALU.mult, op1=ALU.mult),
                  reads=[hkey, "rstd", "gvec"], writes=[outkey])

    def build(self):
        nc = self.nc
        dbg = self.debug
        IN, OUT, INT = "ExternalInput", "ExternalOutput", "Internal"
        xT = self.dram("xT", [D, NT], F32, IN)
        w_in = self.dram("a_w_in", [D, 2048], F32, IN)
        w_ao = self.dram("a_w_out", [D, D], F32, IN)
        w_up = self.dram("w_up", [2, D, 4096], F32, IN)
        w_dn = self.dram("w_down", [2, 4096, D], F32, IN)
        gvecs = self.dram("gvecs", [128, 6, 8], F32, IN)
        lngb = self.dram("lngb", [128, 2, D], F32, IN)
        wsT_d = self.dram("wsT", [128, 8, 128], F32, IN)
        tril_d = self.dram("trilm", [128, 128], F32, IN)
        bsb_d = self.dram("bsb", [128, 8, 128], F32, IN)
        w_qkv = self.dram("b_w_qkv", [D, 4608], F32, IN)
        w_bo = self.dram("b_w_out", [512, D], F32, IN)
        bt_d = self.dram("btab", [128, 24, 2, 128], F32, IN)
        bm_d = self.dram("bmask", [128, 2, 128], F32, IN)
        flag_d = self.dram("hflag", [128, 1], F32, IN)
        ident_d = self.dram("ident", [128, 128], F32, IN)
        sk = OUT if dbg else INT
        xnT = self.dram("xnT", [D, NT], BF16, sk)
        acc_d = self.dram("acc", [NOWN, 528], F32, sk)
        h1T = self.dram("h1T", [D, NT], F32, sk)
        h2T = self.dram("h2T", [D, NT], F32, sk)
        h3T = self.dram("h3T", [D, NOWN], F32, sk)
        outT = self.dram("outT", [D, NOWN], F32, OUT)

        with ExitStack() as st:
            S = self.S = Sched(nc, st)
            sb = lambda name, shape, dt: st.enter_context(nc.sbuf_tensor(name, list(shape), dt))
            self.ps = [st.enter_context(nc.psum_tensor("ps%d" % i, [128, 512], F32)) for i in range(8)]
            self.onesm = sb("onesm", [128, 128], F32)
            self.epst = sb("epst", [128, 1], F32)
            self.gv = sb("gv", [128, 6, 8], F32)
            self.stg = [sb("stg%d" % i, [128, 2048], F32) for i in range(2)]
            self.sqc = [sb("sqc%d" % i, [128, 512], F32) for i in range(2)]
            self.rstd = sb("rstd", [128, 512], F32)
            S.add("pool", lambda e: e.memset(self.onesm[:], 1.0 / D), writes=["onesm"])
            S.add("pool", lambda e: e.memset(self.epst[:], EPS), writes=["epst"])
            S.add("sp", lambda e: e.dma_start(out=self.gv[:], in_=gvecs), writes=["gvec"], dma="const")

            if "A" in self.phases:
                with ExitStack() as pst:
                    self.phase_A(pst, xT, w_in, w_ao, lngb, wsT_d, tril_d, bsb_d, h1T)
                S.barrier()
            if "B0" in self.phases:
                with ExitStack() as pst:
                    self.phase_ffn(pst, 0, h1T, h2T, NT, w_up, w_dn, None, xnT=xnT)
                S.barrier()
            if "C1" in self.phases:
                with ExitStack() as pst:
                    self.phase_C1(pst, h2T, xnT)
                S.barrier()
            if "C2" in self.phases:
                with ExitStack() as pst:
                    self.phase_C2(pst, xnT, w_qkv, bt_d, bm_d, flag_d, acc_d, ident_d)
                S.barrier()
            with ExitStack() as wst:
                pre = self.ffn_weights(wst, 1, w_up, w_dn) if "B1" in self.phases else None
                if "C3" in self.phases:
                    with ExitStack() as pst:
                        self.phase_C3(pst, acc_d, h2T, w_bo, ident_d, h3T, pre[2] if pre else None)
                    S.barrier(keep=[("wup", kc, hf) for kc in range(8) for hf in range(2)] + [("wdn", f2) for f2 in range(16)])
                if "B1" in self.phases:
                    with ExitStack() as pst:
                        self.phase_ffn(pst, 1, h3T, None, NOWN, w_up, w_dn, outT, pre=pre)
                    S.barrier()
            S.emit(final_waits=list(S.dma_cnt.keys()))
        return nc

    def phase_A(self, pst, xT, w_in, w_ao, lngb, wsT_d, tril_d, bsb_d, h1T):
        nc, S = self.nc, self.S
        T = 256
        sb = lambda name, shape, dt: pst.enter_context(nc.sbuf_tensor(name, list(shape), dt))
        win = sb("A_win", [128, 8, 2048], BF16)
        wao = sb("A_wao", [128, 8, D], BF16)
        wsT = sb("A_wsT", [128, 8, 128], BF16)
        wsf = sb("A_wsf", [128, 8, 128], F32)
        trl = sb("A_trl", [128, 128], F32)
        lng = sb("A_lng", [128, 2, D], F32)
        bsb = sb("A_bsb", [128, 8, 128], F32)
        xt = [sb("A_xt%d" % i, [128, 8, T], F32) for i in range(4)]
        xn = [sb("A_xn%d" % i, [128, 8, T], BF16) for i in range(2)]
        uT = [sb("A_uT%d" % i, [128, 8, T], F32) for i in range(2)]
        vtok = [sb("A_vtok%d" % i, [128, D], F32) for i in range(2)]
        vn = [sb("A_vn%d" % i, [128, D], BF16) for i in range(2 * (T // 128))]
        st6 = sb("A_st6", [128, 2, 6], F32)
        mv = sb("A_mv", [128, 2], F32)
        rsv = sb("A_rsv", [128, 1], F32)
        gtmp = [sb("A_gtmp%d" % i, [128, T], F32) for i in range(2)]
        gu = sb("A_gu", [128, 8, T], BF16)

        S.add("sp", lambda e: e.dma_start(out=wsf[:], in_=wsT_d), writes=["wsf"], dma="const")
        S.add("sp", lambda e: e.dma_start(out=trl[:], in_=tril_d), writes=["trl"], dma="const")
        S.add("sp", lambda e: e.dma_start(out=lng[:], in_=lngb), writes=["lng"], dma="const")
        S.add("sp", lambda e: e.dma_start(out=bsb[:], in_=bsb_d), writes=["bsb"], dma="const")
        for g in range(8):
            S.add("pool", lambda e, g=g: e.tensor_tensor(out=wsf[:, g, :], in0=wsf[:, g, :], in1=trl[:], op=ALU.mult),
                  reads=["wsf", "trl"], writes=["wsf"])
        S.add("dve", lambda e: e.tensor_copy(out=wsT[:], in_=wsf[:]), reads=["wsf"], writes=["wsT"])
        for g in range(8):
            b = self.nextps()
            S.add("pe", lambda e, g=g, b=b: e.matmul(self.ps[b][:, 0:128], lhsT=lng[:, 1, g * 128:(g + 1) * 128],
                                                    rhs=wsf[:, g, :], start=True, stop=True),
                  reads=["lng", "wsf"], writes=[("ps", b)])
            S.add("dve", lambda e, g=g, b=b: e.tensor_tensor(out=bsb[:, g, :], in0=self.ps[b][:, 0:128], in1=bsb[:, g, :], op=ALU.add),
                  reads=[("ps", b), "bsb"], writes=["bsb"])
        for kc in range(8):
            self.load_w(w_in[kc * 128:(kc + 1) * 128, :], win[:, kc, :], ("win", kc))
        for kc2 in range(4):
            self.load_w(w_ao[kc2 * 256:(kc2 + 1) * 256, :].rearrange("(c p) n -> p c n", p=128),
                        wao[:, 2 * kc2:2 * kc2 + 2, :], ("wao", kc2))

        ntiles = NT // T
        NJ = T // 128

        def ld(i):
            sl3 = i % 4
            xs = xt[sl3]
            t0 = i * T
            S.add("sp", lambda e, xs=xs, t0=t0: e.dma_start(
                out=xs[:], in_=xT[:, t0:t0 + T].rearrange("(c p) t -> p c t", p=128)),
                writes=[("xt", sl3)], dma="xa%d" % sl3)

        def front(i):
            sl3 = i % 4
            s2 = i % 2
            xs = xt[sl3]
            xnn = xn[s2]
            self.rmsnorm(lambda c, xs=xs: xs[:, c, :], ("xt", sl3), self.gv[:, 0, :], T,
                         lambda c, xnn=xnn: xnn[:, c, :], ("xn", s2), "A")

        def mid(i):
            s2 = i % 2
            xnn = xn[s2]
            uu = uT[s2]
            for f in range(8):
                b = self.nextps()
                ps = self.ps[b]
                for kc in range(8):
                    S.add("pe", lambda e, f=f, kc=kc, ps=ps, xnn=xnn: e.matmul(
                        ps[:, 0:T], lhsT=win[:, kc, f * 128:(f + 1) * 128], rhs=xnn[:, kc, :],
                        start=(kc == 0), stop=(kc == 7)),
                        reads=[("xn", s2), ("win", kc)], writes=[("ps", b)])
                S.add("act", lambda e, f=f, ps=ps, uu=uu: e.activation(out=uu[:, f, :], in_=ps[:, 0:T], func=AF.Gelu),
                      reads=[("ps", b)], writes=[("uT", s2, f)])
            for j in range(NJ):
                vs = (i * NJ + j) % 2
                vt = vtok[vs]
                for hf in range(2):
                    b = self.nextps()
                    ps = self.ps[b]
                    for kc in range(8):
                        S.add("pe", lambda e, j=j, hf=hf, kc=kc, ps=ps, xnn=xnn: e.matmul(
                            ps[:], lhsT=xnn[:, kc, j * 128:(j + 1) * 128],
                            rhs=win[:, kc, 1024 + hf * 512:1024 + (hf + 1) * 512],
                            start=(kc == 0), stop=(kc == 7)),
                            reads=[("xn", s2), ("win", kc)], writes=[("ps", b)])
                    S.add("act", lambda e, hf=hf, ps=ps, vt=vt: e.activation(
                        out=vt[:, hf * 512:(hf + 1) * 512], in_=ps[:], func=AF.Gelu),
                        reads=[("ps", b)], writes=[("vtok", vs)])
                for hf in range(2):
                    S.add("dve", lambda e, hf=hf, vt=vt: e.bn_stats(out=st6[:, hf, :], in_=vt[:, hf * 512:(hf + 1) * 512]),
                          reads=[("vtok", vs)], writes=["st6"])
                S.add("dve", lambda e: e.bn_aggr(out=mv[:], in_=st6[:].rearrange("p a b -> p (a b)")), reads=["st6"], writes=["mv"])
                S.add("act", lambda e: e.activation(out=rsv[:], in_=mv[:, 1:2], func=AF.Sqrt, bias=self.epst[:]),
                      reads=["mv", "epst"], writes=["rsv"])
                S.add("dve", lambda e: e.reciprocal(out=rsv[:], in_=rsv[:]), reads=["rsv"], writes=["rsv"])
                vi = s2 * NJ + j
                vnj = vn[vi]
                S.add("dve", lambda e, vt=vt, vnj=vnj: e.tensor_scalar(out=vnj[:], in0=vt[:], scalar1=mv[:, 0:1], scalar2=rsv[:],
                                                                       op0=ALU.subtract, op1=ALU.mult),
                      reads=[("vtok", vs), "mv", "rsv"], writes=[("vn", vi)])

        def back(i):
            sl3 = i % 4
            s2 = i % 2
            xs = xt[sl3]
            uu = uT[s2]
            t0 = i * T
            for g in range(8):
                b = self.nextps()
                for j in range(NJ):
                    vi = s2 * NJ + j
                    vnj = vn[vi]
                    S.add("pe", lambda e, g=g, j=j, b=b, vnj=vnj: e.matmul(
                        self.ps[b][:, j * 128:(j + 1) * 128], lhsT=vnj[:, g * 128:(g + 1) * 128], rhs=wsT[:, g, :],
                        start=True, stop=True),
                        reads=[("vn", vi), "wsT"], writes=[("ps", b)])
                gs = g % 2
                gt = gtmp[gs]
                for j in range(NJ):
                    S.add("dve", lambda e, g=g, b=b, gt=gt, j=j: e.scalar_tensor_tensor(
                        out=gt[:, j * 128:(j + 1) * 128], in0=self.ps[b][:, j * 128:(j + 1) * 128],
                        scalar=self.gv[:, 5, g:g + 1], in1=bsb[:, g, :], op0=ALU.mult, op1=ALU.add),
                        reads=[("ps", b), "bsb", "gvec"], writes=[("gtmp", gs)])
                S.add("pool", lambda e, g=g, gt=gt, uu=uu: e.tensor_tensor(out=gu[:, g, :], in0=gt[:], in1=uu[:, g, :], op=ALU.mult),
                      reads=[("gtmp", gs), ("uT", s2, g)], writes=[("gu", g)])

        def back_o(i):
            sl3 = i % 4
            xs = xt[sl3]
            t0 = i * T
            for f in range(8):
                b = self.nextps()
                ps = self.ps[b]
                for g in range(8):
                    S.add("pe", lambda e, f=f, g=g, ps=ps: e.matmul(
                        ps[:, 0:T], lhsT=wao[:, g, f * 128:(f + 1) * 128], rhs=gu[:, g, :],
                        start=(g == 0), stop=(g == 7)),
                        reads=[("gu", g), ("wao", g // 2)], writes=[("ps", b)])
                S.add("dve", lambda e, f=f, ps=ps, xs=xs: e.tensor_tensor(out=xs[:, f, :], in0=ps[:, 0:T], in1=xs[:, f, :], op=ALU.add),
                      reads=[("ps", b), ("xt", sl3)], writes=[("xt", sl3)])
            S.add("sp", lambda e, xs=xs, t0=t0: e.dma_start(
                out=h1T[:, t0:t0 + T].rearrange("(c p) t -> p c t", p=128), in_=xs[:]),
                reads=[("xt", sl3)], writes=[("h1T", t0 // 512)], dma="sa%d" % sl3)

        ld(0)
        ld(1)
        front(0)
        for i in range(ntiles):
            if i + 2 < ntiles:
                ld(i + 2)
            if i >= 1:
                back(i - 1)
            if i + 1 < ntiles:
                front(i + 1)
            mid(i)
            if i >= 1:
                back_o(i - 1)
        back(ntiles - 1)
        back_o(ntiles - 1)

    def phase_C1(self, pst, h2T, xnT):
        nc, S = self.nc, self.S
        T = 512
        sb = lambda name, shape, dt: pst.enter_context(nc.sbuf_tensor(name, list(shape), dt))
        ht = [sb("C1_ht%d" % i, [128, 8, T], F32) for i in range(2)]
        xo = [sb("C1_xo%d" % i, [128, 8, T], BF16) for i in range(2)]
        for i in range(NT // T):
            sl = i % 2
            hs, xs = ht[sl], xo[sl]
            t0 = i * T
            S.add("sp", lambda e, hs=hs, t0=t0: e.dma_start(
                out=hs[:], in_=h2T[:, t0:t0 + T].rearrange("(c p) t -> p c t", p=128)),
                reads=[("h2T", i)], writes=[("ht", sl)], dma="xt%d" % sl)
            self.rmsnorm(lambda c, hs=hs: hs[:, c, :], ("ht", sl), self.gv[:, 2, :], T,
                         lambda c, xs=xs: xs[:, c, :], ("xo", sl), "C")
            S.add("sp", lambda e, xs=xs, t0=t0: e.dma_start(
                out=xnT[:, t0:t0 + T].rearrange("(c p) t -> p c t", p=128), in_=xs[:]),
                reads=[("xo", sl)], writes=[("xnT", i)], dma="st%d" % sl)

    def phase_C2(self, pst, xnT, w_qkv, bt_d, bm_d, flag_d, acc_d, ident_d):
        nc, S = self.nc, self.S
        sb = lambda name, shape, dt: pst.enter_context(nc.sbuf_tensor(name, list(shape), dt))
        xn = sb("C_xn", [128, 8, 2048], BF16)
        wsets = [tuple(sb("C_w%s%d" % (nm, i), [128, 8, 512], BF16) for nm in ("q", "k", "v")) for i in range(2)]
        nslots = (2, 2, 5)
        Kq = [[sb("C_K%d_%d" % (g, i), [128, 4, 4, 128], BF16) for i in range(nslots[g])] for g in range(3)]
        Vq = [[sb("C_V%d_%d" % (g, i), [128, 4, 8, 66], BF16) for i in range(nslots[g])] for g in range(3)]
        Qq = [sb("C_Q%d" % i, [128, 4, 2, 4, 128], BF16) for i in range(1)]
        E = sb("C_E", [128, 8, 2, 128], F32)
        bmr = sb("C_bm", [128, 256], F32)
        ktok = sb("C_ktok", [128, 512], BF16)
        qtok = sb("C_qtok", [128, 512], BF16)
        idb = sb("C_idb", [128, 128], BF16)
        flag = sb("C_flag", [128, 1], F32)
        expS = [sb("C_expS%d" % i, [128, 512], F32) for i in range(2)] + [self.sqc[0], self.sqc[1]]
        Pb = [sb("C_P%d" % i, [128, 512], BF16) for i in range(2)]
        accsb = [sb("C_acc%d" % i, [128, 528], F32) for i in range(2)]
        idf = accsb[1][:, 0:128]

        S.add("sp", lambda e: e.dma_start(out=bmr[:], in_=bm_d.rearrange("p b c -> p (b c)")), writes=["bm"], dma="const")
        S.add("sp", lambda e: e.dma_start(out=idf, in_=ident_d), writes=[("accsb", 1)], dma="const")
        S.add("dve", lambda e: e.tensor_copy(out=idb[:], in_=idf), reads=[("accsb", 1)], writes=["idb"])
        S.add("sp", lambda e: e.dma_start(out=flag[:], in_=flag_d), writes=["flag"], dma="const")
        for g in range(3):
            for i in range(nslots[g]):
                S.add("pool", lambda e, g=g, i=i: e.memset(Vq[g][i][:], 1.0), writes=[("V", g, i)])
        for i in range(1):
            S.add("pool", lambda e, i=i: e.memset(Qq[i][:], 0.0), writes=[("Q", i)])

        import os
        lvl = int(os.environ.get("KDBG_C2", "9"))
        cur01 = [0, 0]
        prev01 = [None, None]
        g2slot = [None] * 4
        g2free = [0, 1, 2, 3, 4]
        qrr = 0
        tcount = 0
        dils = (1, 4, 16)
        def group_thunks(k):
            sb_, g_ = k // 3, k % 3
            wq_, wk_, wv_ = wsets[k % 2]
            mats = [(2, wv_, "wv"), (1, wk_, "wk")] + ([(0, wq_, "wq")] if sb_ > 0 else [])
            out = []
            for (qi, wt, wname) in mats:
                c0 = (qi * 3 + g_) * 512
                for k4 in range(2):
                    out.append((w_qkv[k4 * 512:(k4 + 1) * 512, c0:c0 + 512].rearrange("(c p) n -> p c n", p=128),
                                wt[:, 4 * k4:4 * k4 + 4, :], (wname, k % 2, k4)))
            return out

        pend = {"args": [], "casts": []}

        def pf_step():
            if pend["casts"]:
                pend["casts"].pop(0)()
            if pend["args"]:
                d, c = self.load_w_split(*pend["args"].pop(0))
                d()
                pend["casts"].append(c)

        def pf_flush():
            while pend["args"] or pend["casts"]:
                pf_step()

        for sbk in range(3):
            own = sbk > 0
            S.add("sp", lambda e, sbk=sbk: e.dma_start(
                out=xn[:], in_=xnT[:, sbk * 2048:(sbk + 1) * 2048].rearrange("(c p) t -> p c t", p=128)),
                reads=[("xnT", 4 * sbk + j) for j in range(4)], writes=["xnC"], dma="xnC")
            for g in range(3):
                kgrp = sbk * 3 + g
                pf_flush()
                if kgrp == 0:
                    pend["args"] = group_thunks(0)
                    pf_flush()
                if kgrp + 1 < 9:
                    pend["args"] = group_thunks(kgrp + 1)
                wset = kgrp % 2
                wq, wk, wv = wsets[wset]
                if own:
                    S.add("sp", lambda e, g=g: e.dma_start(out=E[:], in_=bt_d[:, g * 8:(g + 1) * 8, :, :]),
                          writes=["E"], dma="Eg")
                    Ef = E[:].rearrange("p a b c -> p (a b c)")
                    S.add("act", lambda e, Ef=Ef: e.activation(out=Ef, in_=Ef, func=AF.Exp), reads=["E"], writes=["E"])
                    S.add("dve", lambda e: e.tensor_tensor(
                        out=E[:].rearrange("p a b c -> p a (b c)"), in0=E[:].rearrange("p a b c -> p a (b c)"),
                        in1=bmr[:].unsqueeze(1).to_broadcast([128, 8, 256]), op=ALU.mult),
                        reads=["E", "bm"], writes=["E"])
                quads = list(range(4)) if (own or g == 2) else [3]
                for q in quads:
                    if g < 2:
                        cs = cur01[g]
                        hs_ = prev01[g]
                        cur01[g] = 1 - cs
                        prev01[g] = cs
                    else:
                        cs = g2free.pop(0)
                        hs_ = g2slot[q]
                        if hs_ is not None:
                            g2free.append(hs_)
                        g2slot[q] = cs
                    Kc, Vc = Kq[g][cs], Vq[g][cs]

                    def xq(kc, g=g, q=q):
                        if g == 0:
                            return xn[:, kc, 512 * q:512 * q + 512].rearrange("p (t i) -> p t i", t=4)
                        if g == 1:
                            return xn[:, kc, 512 * q:512 * q + 512].rearrange("p (i r) -> p r i", r=4)
                        return xn[:, kc, :].rearrange("p (i r) -> p r i", r=16)[:, 4 * q:4 * q + 4, :]

                    if own:
                        qs = 0
                        Qc = Qq[qs]
                    for tt in range(4):
                        b = 4 + self.nextps() % 4
                        ps = self.ps[b]
                        for kc in range(8):
                            S.add("pe", lambda e, tt=tt, kc=kc, ps=ps, xq=xq, wv=wv: e.matmul(
                                ps[:], lhsT=xq(kc)[:, tt, :], rhs=wv[:, kc, :],
                                start=(kc == 0), stop=(kc == 7)),
                                reads=["xnC", ("wv", wset, kc // 4)], writes=[("ps", b)])
                        S.add("dve", lambda e, tt=tt, ps=ps, Vc=Vc: e.tensor_copy(
                            out=Vc[:, tt, :, 0:64], in_=ps[:].rearrange("p (h c) -> p h c", c=64)),
                            reads=[("ps", b)], writes=[("V", g, cs)])
                        bk = 4 + self.nextps() % 4
                        psk = self.ps[bk]
                        for kc in range(8):
                            S.add("pe", lambda e, tt=tt, kc=kc, psk=psk, xq=xq, wk=wk: e.matmul(
                                psk[:], lhsT=xq(kc)[:, tt, :], rhs=wk[:, kc, :],
                                start=(kc == 0), stop=(kc == 7)),
                                reads=["xnC", ("wk", wset, kc // 4)], writes=[("ps", bk)])
                        S.add("dve", lambda e, psk=psk: e.tensor_copy(out=ktok[:], in_=psk[:]),
                              reads=[("ps", bk)], writes=["ktok"])
                        if own:
                            bq = 4 + self.nextps() % 4
                            psq = self.ps[bq]
                            for kc in range(8):
                                S.add("pe", lambda e, tt=tt, kc=kc, psq=psq, xq=xq, wq=wq: e.matmul(
                                    psq[:], lhsT=xq(kc)[:, tt, :], rhs=wq[:, kc, :],
                                    start=(kc == 0), stop=(kc == 7)),
                                    reads=["xnC", ("wq", wset, kc // 4)], writes=[("ps", bq)])
                            S.add("act", lambda e, psq=psq: e.mul(out=qtok[:], in_=psq[:], mul=0.125),
                                  reads=[("ps", bq)], writes=["qtok"])
                        pf_step()
                        bt = 4 + self.nextps() % 4
                        pst_ = self.ps[bt]
                        for c in range(4):
                            S.add("pe", lambda e, c=c, pst_=pst_: e.matmul(
                                pst_[:, c * 128:(c + 1) * 128], lhsT=ktok[:, c * 128:(c + 1) * 128], rhs=idb[:],
                                start=True, stop=True),
                                reads=["ktok", "idb"], writes=[("ps", bt)])
                        S.add("act", lambda e, tt=tt, pst_=pst_, Kc=Kc: e.copy(
                            out=Kc[:, :, tt, :], in_=pst_[:].rearrange("p (c i) -> p c i", c=4)),
                            reads=[("ps", bt)], writes=[("K", g, cs)])
                        if own:
                            bt2 = 4 + self.nextps() % 4
                            pst2 = self.ps[bt2]
                            for c in range(4):
                                S.add("pe", lambda e, c=c, pst2=pst2: e.matmul(
                                    pst2[:, c * 128:(c + 1) * 128], lhsT=qtok[:, c * 128:(c + 1) * 128], rhs=idb[:],
                                    start=True, stop=True),
                                    reads=["qtok", "idb"], writes=[("ps", bt2)])
                            for e_ in range(2):
                                pr = slice(64 * e_, 64 * e_ + 64)
                                eng = "act" if e_ == 0 else "dve"
                                if e_ == 0:
                                    S.add("act", lambda e, tt=tt, pst2=pst2, Qc=Qc, pr=pr, e_=e_: e.copy(
                                        out=Qc[pr, :, e_, tt, :], in_=pst2[pr, :].rearrange("p (c i) -> p c i", c=4)),
                                        reads=[("ps", bt2)], writes=[("Q", qs)])
                                else:
                                    S.add("dve", lambda e, tt=tt, pst2=pst2, Qc=Qc, pr=pr, e_=e_: e.tensor_copy(
                                        out=Qc[pr, :, e_, tt, :], in_=pst2[pr, :].rearrange("p (c i) -> p c i", c=4)),
                                        reads=[("ps", bt2)], writes=[("Q", qs)])
                    if not own or lvl < 2:
                        continue
                    def tile_hist(tt):
                        if g == 0:
                            if tt > 0:
                                return Kc, Vc, tt - 1, cs
                            return Kq[g][hs_], Vq[g][hs_], 3, hs_
                        return Kq[g][hs_], Vq[g][hs_], tt, hs_

                    def emit_S(tt):
                        hK, hV, htile, hkey = tile_hist(tt)
                        for hp in range(4):
                            b = 4 + hp
                            ps = self.ps[b]
                            for e_ in range(2):
                                for blk in range(2):
                                    Ksrc = (hK[:, hp, htile, :] if blk == 0 else Kc[:, hp, tt, :])
                                    col = (e_ * 2 + blk) * 128
                                    S.add("pe", lambda e, Ksrc=Ksrc, e_=e_, hp=hp, tt=tt, ps=ps, col=col, Qc=Qc: e.matmul(
                                        ps[:, col:col + 128], lhsT=Ksrc, rhs=Qc[:, hp, e_, tt, :], start=True, stop=True),
                                        reads=[("K", g, hkey), ("K", g, cs), ("Q", qs)], writes=[("ps", b)])

                    emit_S(0)
                    for tt in range(4):
                        if g == 0:
                            if tt > 0:
                                hK, hV, htile, hkey = Kc, Vc, tt - 1, cs
                            else:
                                hK, hV, htile, hkey = Kq[g][hs_], Vq[g][hs_], 3, hs_
                            hal = (sbk == 1 and q == 0 and tt == 0)
                        else:
                            hK, hV, htile, hkey = Kq[g][hs_], Vq[g][hs_], tt, hs_
                            hal = (sbk == 1 and (g == 2 or q == 0))
                        if os.environ.get("KDBG_NOHAL"):
                            hal = False
                        pvb = (tcount % 2) * 2
                        asl = tcount % 2
                        tcount += 1
                        psA, psB = self.ps[pvb], self.ps[pvb + 1]
                        exs = []
                        for hp in range(4):
                            b = 4 + hp
                            ex = expS[hp]
                            S.add("act", lambda e, b=b, ex=ex: e.activation(out=ex[:], in_=self.ps[b][:], func=AF.Exp),
                                  reads=[("ps", b)], writes=[("expS", hp)])
                            exs.append(ex)
                        if tt + 1 < 4:
                            emit_S(tt + 1)
                        for hp in range(4):
                            b = 4 + hp
                            ps = self.ps[b]
                            es = hp % 2
                            ex, pb = exs[hp], Pb[es]
                            gh = 2 * hp
                            Ev = E[:, gh:gh + 2, :, :]
                            meng = "dve"
                            if not hal:
                                S.add(meng, lambda e, ex=ex, pb=pb, Ev=Ev: e.tensor_tensor(
                                    out=pb[:], in0=ex[:], in1=Ev.rearrange("p a b c -> p (a b c)"), op=ALU.mult),
                                    reads=[("expS", hp), "E"], writes=[("P", es)])
                            else:
                                for e2 in range(2):
                                    c_h = e2 * 256
                                    c_c = e2 * 256 + 128
                                    S.add("dve", lambda e, ex=ex, pb=pb, c_c=c_c, gh=gh, e2=e2: e.tensor_tensor(
                                        out=pb[:, c_c:c_c + 128], in0=ex[:, c_c:c_c + 128], in1=E[:, gh + e2, 1, :], op=ALU.mult),
                                        reads=[("expS", hp), "E"], writes=[("P", es)])
                                    S.add("dve", lambda e, ex=ex, pb=pb, c_h=c_h, gh=gh, e2=e2: e.scalar_tensor_tensor(
                                        out=pb[:, c_h:c_h + 128], in0=ex[:, c_h:c_h + 128], scalar=flag[:, 0:1],
                                        in1=E[:, gh + e2, 0, :], op0=ALU.mult, op1=ALU.mult),
                                        reads=[("expS", hp), "E", "flag"], writes=[("P", es)])
                            for e_ in range(2 if lvl >= 3 else 0):
                                h = 2 * hp + e_
                                pso = psA if h < 4 else psB
                                oc = (h % 4) * 66
                                for blk in range(2):
                                    Vsrc = (hV[:, htile, h, :] if blk == 0 else Vc[:, tt, h, :])
                                    pcol = (e_ * 2 + blk) * 128
                                    S.add("pe", lambda e, pso=pso, oc=oc, pb=pb, pcol=pcol, Vsrc=Vsrc, blk=blk: e.matmul(
                                        pso[:, oc:oc + 66], lhsT=pb[:, pcol:pcol + 128], rhs=Vsrc,
                                        start=(blk == 0), stop=(blk == 1)),
                                        reads=[("P", es), ("V", g, hkey), ("V", g, cs)],
                                        writes=[("ps", pvb if h < 4 else pvb + 1)])
                        if lvl < 3:
                            continue
                        ac = accsb[asl]
                        S.add("act", lambda e, ac=ac, psA=psA: e.copy(out=ac[:, 0:264], in_=psA[:, 0:264]),
                              reads=[("ps", pvb)], writes=[("accsb", asl)])
                        S.add("act", lambda e, ac=ac, psB=psB: e.copy(out=ac[:, 264:528], in_=psB[:, 0:264]),
                              reads=[("ps", pvb + 1)], writes=[("accsb", asl)])
                        so = sbk - 1
                        if g == 0:
                            r0 = so * 2048 + 512 * q + 128 * tt
                            dst = acc_d[r0:r0 + 128, :]
                            S.add("sp", lambda e, ac=ac, dst=dst: e.dma_start(out=dst, in_=ac[:]),
                                  reads=[("accsb", asl)], writes=["accd"], dma="acc%d" % asl)
                        else:
                            if g == 1:
                                dst = acc_d.rearrange("(n i r) c -> n r i c", i=128, r=4)[so * 4 + q, tt]
                            else:
                                dst = acc_d.rearrange("(n i r) c -> n r i c", i=128, r=16)[so, 4 * q + tt]
                            S.add("pool", lambda e, ac=ac, dst=dst: e.dma_start(out=dst, in_=ac[:], accum_op=ALU.add),
                                  reads=[("accsb", asl)], writes=["accd"], dma="accp%d" % asl)

    def phase_C3(self, pst, acc_d, h2T, w_bo, ident_d, h3T, wth=None):
        nc, S = self.nc, self.S
        T = 256
        sb = lambda name, shape, dt: pst.enter_context(nc.sbuf_tensor(name, list(shape), dt))
        wo = sb("M_wo", [128, 4, D], BF16)
        idf = sb("M_idf", [128, 128], F32)
        idb = sb("M_idb", [128, 128], BF16)
        a = [[sb("M_a%d_%d" % (g, i), [128, 528], F32) for i in range(4)] for g in range(1)]
        rdens = [sb("M_rden%d" % i, [128, 8], F32) for i in range(2)]
        otok = [sb("M_otok%d" % i, [128, 512], BF16) for i in range(3)]
        oT = sb("M_oT", [128, 4, T], BF16)
        ht = [sb("M_ht%d" % i, [128, 8, T], F32) for i in range(2)]
        S.add("sp", lambda e: e.dma_start(out=idf[:], in_=ident_d), writes=["idf"], dma="const")
        S.add("pool", lambda e: e.tensor_copy(out=idb[:], in_=idf[:]), reads=["idf"], writes=["idb"])
        for k2 in range(2):
            self.load_w(w_bo[k2 * 256:(k2 + 1) * 256, :].rearrange("(c p) n -> p c n", p=128),
                        wo[:, 2 * k2:2 * k2 + 2, :], ("wo", k2))
        NJ = T // 128
        ntl = NOWN // T

        def loads(i):
            sl = i % 2
            t0 = i * T
            S.add("sp", lambda e, sl=sl, t0=t0: e.dma_start(
                out=ht[sl][:], in_=h2T[:, NHALO + t0:NHALO + t0 + T].rearrange("(c p) t -> p c t", p=128)),
                writes=[("ht", sl)], dma="xt%d" % sl)
            for j in range(NJ):
                asl = (i * NJ + j) % 4
                r0 = t0 + 128 * j
                S.add("sp", lambda e, asl=asl, r0=r0: e.dma_start(out=a[0][asl][:], in_=acc_d[r0:r0 + 128, :]),
                      reads=["accd"], writes=[("a", 0, asl)], dma="a0_%d" % asl)

        loads(0)
        for i in range(ntl):
            sl = i % 2
            hs = ht[sl]
            t0 = i * T
            if i + 1 < ntl:
                loads(i + 1)
            for j in range(NJ):
                cnt = i * NJ + j
                asl = cnt % 4
                rden = rdens[cnt % 2]
                rkey = ("rden", cnt % 2)
                if wth and i >= 1:
                    for _ in range(2):
                        if len(wth) > 16:
                            wth.pop(0)()
                a0 = a[0][asl]
                av = a0[:].rearrange("p (h c) -> p h c", c=66)
                S.add("dve", lambda e, av=av, rden=rden: e.reciprocal(out=rden[:], in_=av[:, :, 64]),
                      reads=[("a", 0, asl)], writes=[rkey])
                ok = otok[cnt % 3]
                okey = ("otok", cnt % 3)
                S.add("dve", lambda e, av=av, ok=ok, rden=rden: e.tensor_tensor(
                    out=ok[:].rearrange("p (h c) -> p h c", c=64), in0=av[:, :, 0:64],
                    in1=rden[:].unsqueeze(2).to_broadcast([128, 8, 64]), op=ALU.mult),
                    reads=[("a", 0, asl), rkey], writes=[okey])
                b = self.nextps()
                ps = self.ps[b]
                for c in range(4):
                    S.add("pe", lambda e, c=c, ps=ps, ok=ok: e.matmul(
                        ps[:, c * 128:(c + 1) * 128], lhsT=ok[:, c * 128:(c + 1) * 128], rhs=idb[:],
                        start=True, stop=True),
                        reads=[okey, "idb"], writes=[("ps", b)])
                S.add("act", lambda e, ps=ps, j=j: e.copy(
                    out=oT[:, :, j * 128:(j + 1) * 128], in_=ps[:].rearrange("p (c t) -> p c t", c=4)),
                    reads=[("ps", b)], writes=[("oT", j)])
            for f in range(8):
                b = self.nextps()
                ps = self.ps[b]
                for c in range(4):
                    S.add("pe", lambda e, f=f, c=c, ps=ps: e.matmul(
                        ps[:, 0:T], lhsT=wo[:, c, f * 128:(f + 1) * 128], rhs=oT[:, c, :],
                        start=(c == 0), stop=(c == 3)),
                        reads=[("oT", j) for j in range(NJ)] + [("wo", c // 2)], writes=[("ps", b)])
                S.add("dve", lambda e, f=f, ps=ps, hs=hs: e.tensor_tensor(out=hs[:, f, :], in0=ps[:, 0:T], in1=hs[:, f, :], op=ALU.add),
                      reads=[("ps", b), ("ht", sl)], writes=[("ht", sl)])
            S.add("sp", lambda e, hs=hs, t0=t0: e.dma_start(
                out=h3T[:, t0:t0 + T].rearrange("(c p) t -> p c t", p=128), in_=hs[:]),
                reads=[("ht", sl)], writes=[("h3T", t0 // 512)], dma="st%d" % sl)

    def ffn_weights(self, pst, l, w_up, w_dn):
        nc = self.nc
        P = "F%d_" % l
        wup = pst.enter_context(nc.sbuf_tensor(P + "wup", [128, 8, 4096], BF16))
        wdn = pst.enter_context(nc.sbuf_tensor(P + "wdn", [128, 32, D], BF16))
        th = []
        for hf in range(2):
            for kc in range(8):
                th.append(lambda kc=kc, hf=hf: self.load_w(
                    w_up[l, kc * 128:(kc + 1) * 128, hf * 2048:(hf + 1) * 2048],
                    wup[:, kc, hf * 2048:(hf + 1) * 2048], ("wup", kc, hf)))
        for f2 in range(16):
            th.append(lambda f2=f2: self.load_w(
                w_dn[l, f2 * 256:(f2 + 1) * 256, :].rearrange("(c p) n -> p c n", p=128),
                wdn[:, 2 * f2:2 * f2 + 2, :], ("wdn", f2)))
        return wup, wdn, th

    def phase_ffn(self, pst, l, hin, hout, ntok, w_up, w_dn, outT, xnT=None, pre=None):
        nc, S = self.nc, self.S
        T = 256
        P = "F%d_" % l
        sb = lambda name, shape, dt: pst.enter_context(nc.sbuf_tensor(P + name, list(shape), dt))
        if pre is None:
            wup, wdn, wth = self.ffn_weights(pst, l, w_up, w_dn)
        else:
            wup, wdn, wth = pre
        ht = [sb("ht%d" % i, [128, 8, T], F32) for i in range(2)]
        xn = [sb("xn%d" % i, [128, 8, T], BF16) for i in range(2)]
        hid = sb("hid", [128, 32, T], BF16)
        rl = [sb("rl%d" % i, [128, T], F32) for i in range(4)]
        ot = [sb("ot%d" % i, [128, 8, T], F32) for i in range(1)] if outT is not None else None
        xo = [sb("xo%d" % i, [128, 8, T], BF16) for i in range(2)] if xnT is not None else None

        while wth and len(wth) > 16:
            wth.pop(0)()
        inkey = "h1T" if l == 0 else "h3T"
        ntiles = ntok // T

        def front(i):
            sl = i % 2
            hs = ht[sl]
            t0 = i * T
            S.add("sp", lambda e, hs=hs, t0=t0: e.dma_start(
                out=hs[:], in_=hin[:, t0:t0 + T].rearrange("(c p) t -> p c t", p=128)),
                reads=[(inkey, t0 // 512)], writes=[("ht", sl)], dma="xt%d" % sl)
            self.rmsnorm(lambda c, hs=hs: hs[:, c, :], ("ht", sl), self.gv[:, 1 + 2 * l, :], T,
                         lambda c, sl=sl: xn[sl][:, c, :], ("xn", sl), "F")

        def up(i):
            sl = i % 2
            for f in range(32):
                b = self.nextps()
                ps = self.ps[b]
                for kc in range(8):
                    S.add("pe", lambda e, f=f, kc=kc, ps=ps, sl=sl: e.matmul(
                        ps[:, 0:T], lhsT=wup[:, kc, f * 128:(f + 1) * 128], rhs=xn[sl][:, kc, :],
                        start=(kc == 0), stop=(kc == 7)),
                        reads=[("xn", sl), ("wup", kc, f // 16)], writes=[("ps", b)])
                rs = f % 4
                r = rl[rs]
                S.add("act", lambda e, ps=ps, r=r: e.activation(out=r[:], in_=ps[:, 0:T], func=AF.Relu),
                      reads=[("ps", b)], writes=[("rl", rs)])
                eng = "pool" if f % 2 == 0 else "dve"
                S.add(eng, lambda e, f=f, r=r: e.tensor_tensor(out=hid[:, f, :], in0=r[:], in1=r[:], op=ALU.mult),
                      reads=[("rl", rs)], writes=[("hid", f)])

        def down(i):
            sl = i % 2
            hs = ht[sl]
            t0 = i * T
            for o in range(8):
                b = self.nextps()
                ps = self.ps[b]
                for f in range(32):
                    S.add("pe", lambda e, f=f, o=o, ps=ps: e.matmul(
                        ps[:, 0:T], lhsT=wdn[:, f, o * 128:(o + 1) * 128], rhs=hid[:, f, :],
                        start=(f == 0), stop=(f == 31)),
                        reads=[("hid", f), ("wdn", f // 2)], writes=[("ps", b)])
                S.add("dve", lambda e, o=o, ps=ps, hs=hs: e.tensor_tensor(out=hs[:, o, :], in0=ps[:, 0:T], in1=hs[:, o, :], op=ALU.add),
                      reads=[("ps", b), ("ht", sl)], writes=[("ht", sl)])
            if hout is not None:
                S.add("sp", lambda e, hs=hs, t0=t0: e.dma_start(
                    out=hout[:, t0:t0 + T].rearrange("(c p) t -> p c t", p=128), in_=hs[:]),
                    reads=[("ht", sl)], writes=[("h2T", t0 // 512)], dma="st%d" % sl)
            if xnT is not None:
                xs = xo[sl]
                self.rmsnorm(lambda c, hs=hs: hs[:, c, :], ("ht", sl), self.gv[:, 2, :], T,
                             lambda c, xs=xs: xs[:, c, :], ("xo", sl), "X")
                S.add("sp", lambda e, xs=xs, t0=t0: e.dma_start(
                    out=xnT[:, t0:t0 + T].rearrange("(c p) t -> p c t", p=128), in_=xs[:]),
                    reads=[("xo", sl)], writes=[("xnT", t0 // 512)], dma="sx%d" % sl)
            if outT is not None:
                os_ = ot[0]
                self.rmsnorm(lambda c, hs=hs: hs[:, c, :], ("ht", sl), self.gv[:, 4, :], T,
                             lambda c, os_=os_: os_[:, c, :], ("ot", 0), "O")
                S.add("sp", lambda e, os_=os_, t0=t0: e.dma_start(
                    out=outT[:, t0:t0 + T].rearrange("(c p) t -> p c t", p=128), in_=os_[:]),
                    reads=[("ot", 0)], dma="so")

        front(0)
        while wth:
            wth.pop(0)()
        for i in range(ntiles):
            up(i)
            if i + 1 < ntiles:
                front(i + 1)
            down(i)


def host_common(inp):
    f = lambda a: np.ascontiguousarray(np.asarray(a, dtype=np.float32))
    gv = np.stack([inp["mix_norm_g"][0], inp["mlp_norm_g"][0], inp["mix_norm_g"][1],
                   inp["mlp_norm_g"][1], inp["final_norm_g"], inp["a_ln_g"][0]], axis=0)
    gv = gv.reshape(6, 8, 128).transpose(2, 0, 1)
    lngb = np.stack([np.broadcast_to(inp["a_ln_g"][0], (128, D)),
                     np.broadcast_to(inp["a_ln_b"][0], (128, D))], axis=1)
    wsT = np.transpose(inp["a_w_s"][0], (2, 0, 1))
    tril = np.triu(np.ones((128, 128), np.float32))
    bsb = np.broadcast_to(inp["a_b_s"][0][None], (128, 8, 128))
    ii = np.arange(128)[None, :]
    kk = np.arange(128)[:, None]
    rel = np.stack([128 + ii - kk, ii - kk], axis=1)
    bmask = ((rel >= 0) & (rel <= 128)).astype(np.float32)
    rb = np.asarray(inp["rel_bias"], np.float32)
    bt = np.empty((128, 24, 2, 128), np.float32)
    for g, dil in enumerate((1, 4, 16)):
        dist = np.clip(rel, 0, 128) * dil
        nf = np.maximum(dist, 1).astype(np.float32)
        large = 16 + (np.log(nf / np.float32(16)) / np.float32(math.log(2048 / 16)) * np.float32(16)).astype(np.int32)
        large = np.minimum(large, 31)
        bucket = np.where(dist < 16, dist, large)
        bt[:, g * 8:(g + 1) * 8] = np.transpose(rb[bucket][..., g * 8:(g + 1) * 8], (0, 3, 1, 2))
    return {
        "b_w_qkv": f(inp["b_w_qkv"][0]), "b_w_out": f(inp["b_w_out"][0]),
        "btab": f(bt), "bmask": f(bmask), "ident": np.eye(128, dtype=np.float32),
        "a_w_in": f(inp["a_w_in"][0]), "a_w_out": f(inp["a_w_out"][0]),
        "w_up": f(inp["w_up"]), "w_down": f(inp["w_down"]),
        "gvecs": f(gv), "lngb": f(lngb), "wsT": f(wsT), "trilm": f(tril), "bsb": f(bsb),
    }


def host_core(x, c):
    b, half = c // 2, c % 2
    xt = np.zeros((D, NT), np.float32)
    s0 = half * NOWN
    xt[:, NHALO:] = x[b, s0:s0 + NOWN].T
    if half == 1:
        xt[:, :NHALO] = x[b, s0 - NHALO:s0].T
    return {"xT": xt, "hflag": np.full((128, 1), float(half), np.float32)}


_NC_CACHE = {}


def kernel(**inputs):
    inp = {k: np.asarray(v) for k, v in inputs.items()}
    if "nc" not in _NC_CACHE:
        _NC_CACHE["nc"] = Builder().build()
    nc = _NC_CACHE["nc"]
    common = host_common(inp)
    x = np.asarray(inp["x"], np.float32)
    in_maps = []
    for c in range(8):
        m = dict(common)
        m.update(host_core(x, c))
        in_maps.append(m)
    res = run_bass_kernel_spmd(nc, in_maps, core_ids=list(range(8)))
    out = np.empty((4, 8192, D), np.float32)
    for c in range(8):
        b, half = c // 2, c % 2
        out[b, half * NOWN:(half + 1) * NOWN] = res.results[c]["outT"].T
    return out
```
